# Optimizing a Trainium2 kernel written in Bass

```python
import math
import jax, jax.numpy as jnp
from jax import lax
import numpy as np

D_MODEL = 1024
BATCH = 32
SEQ = 2048
DEPTH = 1
DEC_BATCH = 16
DEC_SEQ = 4096
PAST_LEN = 128

D_MIX = D_MODEL
D_SSM = D_MIX // 2
D_ATTN = D_MIX - D_SSM
SSM_GROUP = 16
N_SSM_GROUPS = D_SSM // SSM_GROUP
SSM_STATE = 64
HEAD_DIM = 64
N_HEADS = D_ATTN // HEAD_DIM
N_KV_HEADS = 2
GQA_GROUP = N_HEADS // N_KV_HEADS
WINDOW = 128
BLOCK = 128
N_BUCKETS = 32
MAX_DISTANCE = 128
RMS_EPS = 1e-6
NEG_INF = -1e30
STEP_MIN = 0.001
STEP_MAX = 0.1
D_IN_PROJ = 2 * D_SSM + D_ATTN + 2 * N_KV_HEADS * HEAD_DIM + D_ATTN
SPLITS = (D_SSM, 2 * D_SSM, 2 * D_SSM + D_ATTN,
          2 * D_SSM + D_ATTN + N_KV_HEADS * HEAD_DIM,
          2 * D_SSM + D_ATTN + 2 * N_KV_HEADS * HEAD_DIM)

kernel_name = "hymba_s5_window_gqa_encoder"


def _rmsnorm(x, w):
    xf = x.astype(jnp.float32)
    y = xf * lax.rsqrt(jnp.mean(xf * xf, axis=-1, keepdims=True) + RMS_EPS)
    return (y * w.astype(jnp.float32)).astype(x.dtype)


def _t5_buckets(rel):
    half = N_BUCKETS // 2
    max_exact = half // 2
    ret = jnp.where(rel > 0, half, 0)
    n = jnp.abs(rel)
    nf = jnp.maximum(n, 1).astype(jnp.float32)
    large = max_exact + (jnp.log(nf / max_exact) / math.log(MAX_DISTANCE / max_exact)
                         * (half - max_exact)).astype(jnp.int32)
    large = jnp.minimum(large, half - 1)
    return ret + jnp.where(n < max_exact, n, large)


def _ssm_direction(u, lam_re, lam_im, log_step, b_re, b_im, c_re, c_im, reverse):
    f32 = jnp.float32
    lam_re = lam_re.astype(f32); lam_im = lam_im.astype(f32)
    dt = jnp.exp(log_step.astype(f32))[:, None]
    mag = jnp.exp(lam_re * dt)
    lb_re = mag * jnp.cos(lam_im * dt)
    lb_im = mag * jnp.sin(lam_im * dt)
    den = lam_re * lam_re + lam_im * lam_im
    nr = lb_re - 1.0
    coef_re = ((nr * lam_re + lb_im * lam_im) / den)[..., None]
    coef_im = ((lb_im * lam_re - nr * lam_im) / den)[..., None]
    b_re = b_re.astype(f32); b_im = b_im.astype(f32)
    bb_re = coef_re * b_re - coef_im * b_im
    bb_im = coef_re * b_im + coef_im * b_re
    bu_re = jnp.einsum('blgh,gph->blgp', u, bb_re)
    bu_im = jnp.einsum('blgh,gph->blgp', u, bb_im)
    a_re = jnp.broadcast_to(lb_re, bu_re.shape)
    a_im = jnp.broadcast_to(lb_im, bu_im.shape)

    def combine(e1, e2):
        a1r, a1i, b1r, b1i = e1
        a2r, a2i, b2r, b2i = e2
        return (a2r * a1r - a2i * a1i,
                a2r * a1i + a2i * a1r,
                a2r * b1r - a2i * b1i + b2r,
                a2r * b1i + a2i * b1r + b2i)

    _, _, s_re, s_im = lax.associative_scan(combine, (a_re, a_im, bu_re, bu_im),
                                            axis=1, reverse=reverse)
    return (jnp.einsum('blgp,ghp->blgh', s_re, c_re.astype(f32))
            - jnp.einsum('blgp,ghp->blgh', s_im, c_im.astype(f32)))


def _ssm_branch(u, lam_re, lam_im, log_step, b_re, b_im, c_re, c_im, d_skip, w_glu, b_glu):
    b, l, _ = u.shape
    ug = u.reshape(b, l, N_SSM_GROUPS, SSM_GROUP).astype(jnp.float32)
    y = d_skip.astype(jnp.float32) * ug
    for d in range(2):
        y = y + _ssm_direction(ug, lam_re[d], lam_im[d], log_step[d], b_re[d], b_im[d],
                               c_re[d], c_im[d], reverse=(d == 1))
    y = y.reshape(b, l, D_SSM).astype(u.dtype)
    g = jax.nn.gelu(y)
    return g * jax.nn.sigmoid(g @ w_glu + b_glu)


def _attn_branch(q, k, v, sink, rel_bias):
    b, l, _ = q.shape
    nb = l // BLOCK
    qb = q.reshape(b, nb, BLOCK, N_KV_HEADS, GQA_GROUP, HEAD_DIM)

    def band(t):
        tp = jnp.pad(t.reshape(b, l, N_KV_HEADS, HEAD_DIM), ((0, 0), (BLOCK, BLOCK), (0, 0), (0, 0)))
        tp = tp.reshape(b, nb + 2, BLOCK, N_KV_HEADS, HEAD_DIM)
        return jnp.concatenate([tp[:, :-2], tp[:, 1:-1], tp[:, 2:]], axis=2)

    kw = band(k)
    vw = band(v)
    q_idx = jnp.arange(BLOCK, dtype=jnp.int32)[:, None]
    s_idx = jnp.arange(3 * BLOCK, dtype=jnp.int32)[None, :]
    rel = s_idx - BLOCK - q_idx
    bias = rel_bias.astype(jnp.float32)[_t5_buckets(rel)]
    bias = jnp.transpose(bias, (2, 0, 1)).reshape(N_KV_HEADS, GQA_GROUP, BLOCK, 3 * BLOCK)
    kpos = jnp.arange(nb, dtype=jnp.int32)[:, None, None] * BLOCK + s_idx[None] - BLOCK
    valid = (jnp.abs(rel) <= WINDOW)[None] & (kpos >= 0) & (kpos < l)

    scores = jnp.einsum('bnqkgd,bnskd->bnkgqs', qb, kw, preferred_element_type=jnp.float32)
    scores = scores * (HEAD_DIM ** -0.5) + bias
    scores = jnp.where(valid[None, :, None, None], scores, NEG_INF)
    sink_l = sink.astype(jnp.float32).reshape(N_KV_HEADS, GQA_GROUP, 1, 1)
    m = jnp.maximum(jnp.max(scores, axis=-1, keepdims=True), sink_l)
    p = jnp.exp(scores - m)
    denom = jnp.sum(p, axis=-1, keepdims=True) + jnp.exp(sink_l - m)
    probs = (p / denom).astype(v.dtype)
    out = jnp.einsum('bnkgqs,bnskd->bnqkgd', probs, vw)
    return out.reshape(b, l, D_ATTN)


def _layer(x, norm_w, w_in, lam_re, lam_im, log_step, b_re, b_im, c_re, c_im, d_skip,
           w_glu, b_glu, ssm_norm_w, sink, attn_norm_w, w_out, rel_bias):
    h = _rmsnorm(x, norm_w)
    proj = h @ w_in
    u_ssm, z_ssm, q, k, v, z_attn = jnp.split(proj, SPLITS, axis=-1)
    y_ssm = _ssm_branch(u_ssm, lam_re, lam_im, log_step, b_re, b_im, c_re, c_im, d_skip, w_glu, b_glu)
    y_attn = _attn_branch(q, k, v, sink, rel_bias)
    mixed = jnp.concatenate([_rmsnorm(y_ssm, ssm_norm_w) * jax.nn.silu(z_ssm),
                             _rmsnorm(y_attn, attn_norm_w) * jax.nn.silu(z_attn)], axis=-1)
    return x + mixed @ w_out


def _encoder(x, norm_w, w_in, lam_re, lam_im, log_step, b_re, b_im, c_re, c_im, d_skip,
             w_glu, b_glu, ssm_norm_w, sink, attn_norm_w, w_out, rel_bias, final_norm_w):
    for i in range(DEPTH):
        x = _layer(x, norm_w[i], w_in[i], lam_re[i], lam_im[i], log_step[i], b_re[i], b_im[i],
                   c_re[i], c_im[i], d_skip[i], w_glu[i], b_glu[i], ssm_norm_w[i], sink[i],
                   attn_norm_w[i], w_out[i], rel_bias)
    return _rmsnorm(x, final_norm_w)


def setup_inputs(seed: int = 0) -> dict:
    key = jax.random.key(seed)
    ks = jax.random.split(key, 24)
    f32 = jnp.float32
    G, P, H = N_SSM_GROUPS, SSM_STATE, SSM_GROUP
    nrm = lambda k, s: jax.random.normal(k, s, f32)
    x_prompt = nrm(ks[0], (BATCH, SEQ, D_MODEL))
    x_sample = nrm(ks[1], (DEC_BATCH, DEC_SEQ, D_MODEL))
    norm_w = 1.0 + 0.02 * nrm(ks[2], (DEPTH, D_MODEL))
    w_in = nrm(ks[3], (DEPTH, D_MODEL, D_IN_PROJ)) * D_MODEL ** -0.5
    lam_re = -0.5 + 0.01 * nrm(ks[4], (DEPTH, 2, G, P))
    lam_im = math.pi * jnp.arange(P, dtype=f32) + 0.01 * nrm(ks[5], (DEPTH, 2, G, P))
    log_step = jax.random.uniform(ks[6], (DEPTH, 2, G), f32, math.log(STEP_MIN), math.log(STEP_MAX))
    b_re = nrm(ks[7], (DEPTH, 2, G, P, H)) * (2 * H) ** -0.5
    b_im = nrm(ks[8], (DEPTH, 2, G, P, H)) * (2 * H) ** -0.5
    c_re = nrm(ks[9], (DEPTH, 2, G, H, P)) * (2 * P) ** -0.5
    c_im = nrm(ks[10], (DEPTH, 2, G, H, P)) * (2 * P) ** -0.5
    d_skip = nrm(ks[11], (DEPTH, G, H))
    w_glu = nrm(ks[12], (DEPTH, D_SSM, D_SSM)) * D_SSM ** -0.5
    b_glu = 0.02 * nrm(ks[13], (DEPTH, D_SSM))
    ssm_norm_w = 1.0 + 0.02 * nrm(ks[14], (DEPTH, D_SSM))
    sink = 0.5 * nrm(ks[15], (DEPTH, N_HEADS))
    attn_norm_w = 1.0 + 0.02 * nrm(ks[16], (DEPTH, D_ATTN))
    w_out = nrm(ks[17], (DEPTH, D_MIX, D_MODEL)) * D_MIX ** -0.5
    rel_bias = 0.1 * nrm(ks[18], (N_BUCKETS, N_HEADS))
    final_norm_w = 1.0 + 0.02 * nrm(ks[19], (D_MODEL,))
    return {"x_prompt": x_prompt, "x_sample": x_sample, "norm_w": norm_w, "w_in": w_in,
            "lam_re": lam_re, "lam_im": lam_im, "log_step": log_step, "b_re": b_re, "b_im": b_im,
            "c_re": c_re, "c_im": c_im, "d_skip": d_skip, "w_glu": w_glu, "b_glu": b_glu,
            "ssm_norm_w": ssm_norm_w, "sink": sink, "attn_norm_w": attn_norm_w, "w_out": w_out,
            "rel_bias": rel_bias, "final_norm_w": final_norm_w}


def reference(x_prompt, x_sample, norm_w, w_in, lam_re, lam_im, log_step, b_re, b_im, c_re, c_im,
              d_skip, w_glu, b_glu, ssm_norm_w, sink, attn_norm_w, w_out, rel_bias, final_norm_w):
    y_prompt = _encoder(x_prompt, norm_w, w_in, lam_re, lam_im, log_step, b_re, b_im, c_re, c_im,
                        d_skip, w_glu, b_glu, ssm_norm_w, sink, attn_norm_w, w_out, rel_bias, final_norm_w)
    y_sample = _encoder(x_sample, norm_w, w_in, lam_re, lam_im, log_step, b_re, b_im, c_re, c_im,
                        d_skip, w_glu, b_glu, ssm_norm_w, sink, attn_norm_w, w_out, rel_bias, final_norm_w)
    return (y_prompt, y_sample)
```

```python
from contextlib import ExitStack
import numpy as np
import concourse.bass as bass
import concourse.mybir as mybir
from concourse.bass_utils import run_bass_kernel_spmd

F32 = mybir.dt.float32
BF16 = mybir.dt.bfloat16
I32 = mybir.dt.int32
AF = mybir.ActivationFunctionType
ALU = mybir.AluOpType

D = 1024
DP = 2304
PJ = 1664
PI = float(np.pi)
TWO_PI = float(2 * np.pi)


class Tok:
    __slots__ = ("sem", "val", "key")

    def __init__(self, sem, val, key):
        self.sem, self.val, self.key = sem, val, key


class Res:
    __slots__ = ("w", "r")

    def __init__(self):
        self.w = None
        self.r = []


class Prog:
    ENGS = ("sync", "scalar", "vector", "gpsimd", "tensor")

    def __init__(self, nc, stack):
        self.nc = nc
        self.stack = stack
        self.lists = {e: [] for e in self.ENGS}
        self.esem = {e: stack.enter_context(nc.semaphore("es_" + e)) for e in self.ENGS}
        self.ecnt = {e: 0 for e in self.ENGS}
        self.seen = {e: {} for e in self.ENGS}
        self.dsems = {}

    def _deps(self, reads, writes):
        deps = []
        for r in reads:
            if r.w is not None:
                deps.append(r.w)
        for w in writes:
            if w.w is not None:
                deps.append(w.w)
            deps.extend(w.r)
        return deps

    def _waits(self, eng, deps):
        best = {}
        for d in deps:
            if d is None:
                continue
            if d.key not in best or best[d.key].val < d.val:
                best[d.key] = d
        for k, d in best.items():
            if self.seen[eng].get(k, 0) < d.val:
                self.lists[eng].append(("w", d.sem, d.val))
                self.seen[eng][k] = d.val

    def _mark(self, tok, reads, writes):
        for r in reads:
            r.r.append(tok)
        for w in writes:
            w.w = tok
            w.r = []

    def op(self, eng, fn, reads=(), writes=(), extra=()):
        self._waits(eng, self._deps(reads, writes) + list(extra))
        self.ecnt[eng] += 1
        t = Tok(self.esem[eng], self.ecnt[eng], "e_" + eng)
        self.lists[eng].append(("i", fn, self.esem[eng], 1))
        self._mark(t, reads, writes)
        return t

    def group(self, eng, fns, reads=(), writes=()):
        self._waits(eng, self._deps(reads, writes))
        for fn in fns[:-1]:
            self.lists[eng].append(("i", fn, None, 0))
        self.ecnt[eng] += 1
        t = Tok(self.esem[eng], self.ecnt[eng], "e_" + eng)
        self.lists[eng].append(("i", fns[-1], self.esem[eng], 1))
        self._mark(t, reads, writes)
        return t

    def dma(self, eng, fn, dname, reads=(), writes=()):
        if dname not in self.dsems:
            self.dsems[dname] = [self.stack.enter_context(self.nc.semaphore("ds_" + dname)), 0]
        self._waits(eng, self._deps(reads, writes))
        ds = self.dsems[dname]
        ds[1] += 16
        self.lists[eng].append(("i", fn, ds[0], 16))
        t = Tok(ds[0], ds[1], "d_" + dname)
        self._mark(t, reads, writes)
        return t

    def barrier(self):
        toks = [Tok(self.esem[e], self.ecnt[e], "e_" + e) for e in self.ENGS if self.ecnt[e] > 0]
        toks += [Tok(ds[0], ds[1], "d_" + nm) for nm, ds in self.dsems.items()]
        for e in self.ENGS:
            self._waits(e, toks)

    def wait_all(self, eng, toks):
        self._waits(eng, toks)

    def replay(self):
        nc = self.nc
        lists = self.lists

        def run(e, items):
            pend = []
            for it in items:
                if it[0] == "w":
                    pend.append(it)
                    continue
                for w in pend[:-1]:
                    e.wait_ge(w[1], w[2])
                ins = it[1](e)
                if pend:
                    ins._wait_ge(pend[-1][1], pend[-1][2])
                pend = []
                if it[2] is not None:
                    ins.then_inc(it[2], it[3])
            for w in pend:
                e.wait_ge(w[1], w[2])

        with nc.Block() as block:
            @block.sync
            def _(e):
                run(e, lists["sync"])

            @block.scalar
            def _(e):
                run(e, lists["scalar"])

            @block.vector
            def _(e):
                run(e, lists["vector"])

            @block.gpsimd
            def _(e):
                run(e, lists["gpsimd"])

            @block.tensor
            def _(e):
                run(e, lists["tensor"])


def mkap(t, offset, dims):
    tens = t.tensor if hasattr(t, "tensor") else t
    return bass.AP(tens, offset, [list(d) for d in dims])


def sub(ap, extra_off, dims):
    return mkap(ap, ap.offset + extra_off, [list(ap.ap[0])] + [list(d) for d in dims])


def t5_bucket_np(rel):
    half = 16
    max_exact = 8
    ret = np.where(rel > 0, half, 0)
    n = np.abs(rel)
    nf = np.maximum(n, 1).astype(np.float32)
    large = max_exact + (np.log(nf / np.float32(max_exact)) / np.float32(np.log(128 / max_exact))
                         * np.float32(half - max_exact)).astype(np.int32)
    large = np.minimum(large, half - 1)
    return ret + np.where(n < max_exact, n, large)


def host_consts():
    c = {}
    import ml_dtypes
    c["c_ident"] = np.eye(128, dtype=np.float32)
    c["c_exch"] = np.eye(128, dtype=np.float32)[::-1].copy()
    jj = np.arange(128) // 16
    c["c_maskf"] = (jj[None, :] >= jj[:, None]).astype(np.float32)
    c["c_maskb"] = (jj[None, :] <= jj[:, None]).astype(np.float32)
    ng = np.zeros((128, 2, 40), np.float32)
    ng[:, 0, :] = np.arange(-7, 33)[None, :]
    ng[:, 1, :] = (32 - np.arange(40))[None, :]
    c["c_ng"] = ng.reshape(128, 80)
    c["c_cg"] = np.tile(np.arange(128, dtype=np.float32)[None, :], (128, 1))
    i = np.arange(511)
    delta = 255 - i
    bk = t5_bucket_np(delta)
    oh = np.zeros((32, 512), np.float32)
    oh[bk, i] = 1.0
    c["c_oh"] = oh
    valid = np.zeros((8, 512), np.float32)
    valid[:, :511] = (np.abs(delta) <= 128).astype(np.float32)[None, :]
    c["c_valid"] = valid
    return c


def build_nc(seqs, upto=9):
    NTOK = sum(seqs)
    NB = NTOK // 128
    NC = NTOK // 32
    NCT = NC // 128
    assert NC % 128 == 0
    blk_seq = []
    for si, L in enumerate(seqs):
        blk_seq += [si] * (L // 128)
    segs = []
    cs = 0
    i = 0
    while i < len(seqs):
        j = i
        while j < len(seqs) and seqs[j] == seqs[i]:
            j += 1
        segs.append((cs, j - i, seqs[i] // 32))
        cs += (j - i) * (seqs[i] // 32)
        i = j
    NSEQ = len(seqs)
    PADC = NC + NSEQ

    nc = bass.Bass("TRN2", target_bir_lowering=False)
    dt_in = lambda n, s, d=F32: nc.dram_tensor(n, list(s), d, kind="ExternalInput").ap()
    x = dt_in("x", [NTOK, D])
    norm_w = dt_in("norm_w", [1, D])
    w_in = dt_in("w_in", [D, DP])
    lam_re = dt_in("lam_re", [2, 32, 64])
    lam_im = dt_in("lam_im", [2, 32, 64])
    log_step = dt_in("log_step", [1, 64])
    b_re = dt_in("b_re", [2, 32, 64, 16])
    b_im = dt_in("b_im", [2, 32, 64, 16])
    c_re = dt_in("c_re", [2, 32, 16, 64])
    c_im = dt_in("c_im", [2, 32, 16, 64])
    d_skip = dt_in("d_skip", [1, 512])
    w_glu = dt_in("w_glu", [512, 512])
    b_glu = dt_in("b_glu", [1, 512])
    ssm_norm_w = dt_in("ssm_norm_w", [1, 512])
    sink = dt_in("sink", [1, 8])
    attn_norm_w = dt_in("attn_norm_w", [1, 512])
    w_out = dt_in("w_out", [D, D])
    rel_bias = dt_in("rel_bias", [32, 8])
    final_norm_w = dt_in("final_norm_w", [1, D])
    c_ident = dt_in("c_ident", [128, 128])
    c_exch = dt_in("c_exch", [128, 128])
    c_maskf = dt_in("c_maskf", [128, 128])
    c_maskb = dt_in("c_maskb", [128, 128])
    c_ng = dt_in("c_ng", [128, 80])
    c_cg = dt_in("c_cg", [128, 128])
    c_oh = dt_in("c_oh", [32, 512])
    c_valid = dt_in("c_valid", [8, 512])
    y = nc.dram_tensor("y", [NTOK, D], F32, kind="ExternalOutput").ap()
    proj = nc.dram_tensor("proj", [NTOK, PJ], BF16, kind="Internal").ap()
    qkT = nc.dram_tensor("qkT", [768, NTOK], BF16, kind="Internal").ap()
    gd = nc.dram_tensor("gd", [NTOK, 512], BF16, kind="Internal").ap()
    wd = nc.dram_tensor("wd", [8, 512], F32, kind="Internal").ap()

    with ExitStack() as st:
        P = Prog(nc, st)
        sb = lambda n, s, d=F32, stk=st: stk.enter_context(nc.sbuf_tensor(n, list(s), d))
        ps = lambda n, s, d=F32, stk=st: stk.enter_context(nc.psum_tensor(n, list(s), d))

        def bcast_rows(src, n):
            return mkap(src, 0, [[0, 128], [1, n]])

        ident_f = sb("ident_f", [128, 128])
        ident_b = sb("ident_b", [128, 128], BF16)
        R_ident = Res()
        P.dma("sync", lambda e: e.dma_start(out=ident_f[:], in_=c_ident[:, :]), "c0", writes=[R_ident])
        P.op("vector", lambda e: e.tensor_copy(out=ident_b[:], in_=ident_f[:]), reads=[R_ident], writes=[R_ident])
        R_proj = [Res() for _ in range(NB)]
        R_qkT = [Res() for _ in range(NB)]
        R_gdp = [[[Res() for _ in range(4)] for _ in range(8)] for _ in range(4)]

        sT = ExitStack()
        sbT = lambda n, s, d=F32: sb(n, s, d, sT)
        V = "vector"
        LR = sbT("LR", [128, 64]); LI = sbT("LI", [128, 64]); DT = sbT("DT", [128, 64])
        AR = sbT("AR", [128, 64]); TH = sbT("TH", [128, 64])
        DSK = sbT("DSK", [128, 512])
        NG = sbT("NG", [128, 2, 40]); CG = sbT("CG", [128, 128])
        MASKF = sbT("MASKF", [128, 128]); MASKB = sbT("MASKB", [128, 128])
        PRE = sbT("PRE", [128, 64, 40]); PIM = sbT("PIM", [128, 64, 40])
        X1 = sbT("X1", [128, 64, 16]); X2 = sbT("X2", [128, 64, 16])
        Y1 = sbT("Y1", [128, 64, 16]); Y2 = sbT("Y2", [128, 64, 16])
        M32 = sbT("M32", [128, 64]); TH32 = sbT("TH32", [128, 64])
        R_T = Res()
        with ExitStack() as s1:
            sb1 = lambda n, s, d=F32: sb(n, s, d, s1)
            ps1 = lambda n, s, d=F32: ps(n, s, d, s1)
            w_in_bf = sb1("w_in_bf", [128, 8, DP], BF16)
            wstage = sb1("wstage", [128, DP])
            nwT = sb1("nwT", [128, 8])
            R_w, R_ws, R_nw = Res(), Res(), Res()
            P.dma("sync", lambda e: e.dma_start(out=nwT[:], in_=mkap(norm_w, 0, [[1, 128], [128, 8]]),
                                                allow_slow_non_contiguous=True), "c1", writes=[R_nw])
            for kc in range(8):
                P.dma("sync", lambda e, kc=kc: e.dma_start(out=wstage[:], in_=w_in[kc * 128:(kc + 1) * 128, :]),
                      "ws", writes=[R_ws])
                P.op("vector", lambda e, kc=kc: e.tensor_scalar(out=w_in_bf[:, kc, :], in0=wstage[:],
                                                                scalar1=nwT[:, kc:kc + 1], scalar2=None, op0=ALU.mult),
                     reads=[R_ws, R_nw], writes=[R_w])
            w_kd = sb1("w_kd", [128, 8, 2, 128], BF16)
            for kvh in range(2):
                for cpy in range(2):
                    P.op("vector", lambda e, kvh=kvh, cpy=cpy: e.tensor_copy(
                        out=w_kd[:, :, kvh, cpy * 64:(cpy + 1) * 64], in_=w_in_bf[:, :, 1536 + kvh * 64:1536 + (kvh + 1) * 64]),
                         reads=[R_w], writes=[R_w])
            NX = 3
            xt = [sb1(f"xt{i}", [128, D]) for i in range(NX)]
            xb = [sb1(f"xb{i}", [128, D], BF16) for i in range(2)]
            xT = [sb1(f"xT{i}", [128, 8, 128], BF16) for i in range(2)]
            pj = [sb1(f"pj{i}", [128, PJ], BF16) for i in range(2)]
            qk = [sb1(f"qk{i}", [128, 6, 128], BF16) for i in range(2)]
            junk = [sb1(f"junk{i}", [128, D], BF16) for i in range(2)]
            ss = [sb1(f"ss{i}", [128, 2]) for i in range(2)]
            R_xt = [Res() for _ in range(NX)]
            R_xb, R_xT, R_pj, R_qk, R_junk, R_ss = ([Res(), Res()] for _ in range(6))
            ptr = [ps1(f"ptr1{i}", [128, 1024], BF16) for i in range(2)]
            R_ptr = [Res(), Res()]
            pp = [ps1(f"pp{i}", [128, 512]) for i in range(4)]
            R_pp = [Res() for _ in range(4)]
            pq = ps1("pq", [128, 1024])
            R_pq = [Res(), Res()]
            colblocks = [(0, 512, 0), (512, 512, 512), (1664, 512, 1024), (2176, 128, 1536)]

            def s1_load(b):
                lx = b % NX
                return [lambda: P.dma("sync", lambda e: e.dma_start(out=xt[lx][:], in_=x[b * 128:(b + 1) * 128, :]),
                                      f"x{lx}", writes=[R_xt[lx]])]

            def s1_block(b):
                T = []
                s = b % 2
                lx = b % NX
                T.append(lambda: P.op("vector", lambda e: e.memset(ss[s][:], 0.0), writes=[R_ss[s]]))
                T.append(lambda: P.op("scalar", lambda e: e.activation(out=junk[s][:], in_=xt[lx][:], func=AF.Square, accum_out=ss[s][:, 0:1]),
                                      reads=[R_xt[lx]], writes=[R_junk[s], R_ss[s]]))
                T.append(lambda: P.op("scalar", lambda e: e.activation(out=ss[s][:, 1:2], in_=ss[s][:, 0:1], func=AF.Ln, scale=1.0 / D, bias=1e-6),
                                      reads=[R_ss[s]], writes=[R_ss[s]]))
                T.append(lambda: P.op("scalar", lambda e: e.activation(out=ss[s][:, 1:2], in_=ss[s][:, 1:2], func=AF.Exp, scale=-0.5),
                                      reads=[R_ss[s]], writes=[R_ss[s]]))
                T.append(lambda: P.op("vector", lambda e: e.tensor_scalar(out=xb[s][:], in0=xt[lx][:], scalar1=ss[s][:, 1:2], scalar2=None, op0=ALU.mult),
                                      reads=[R_xt[lx], R_ss[s]], writes=[R_xb[s]]))
                T.append(lambda: P.group("tensor", [(lambda e, k=k: e.transpose(out=ptr[s][:, k * 128:(k + 1) * 128], in_=xb[s][:, k * 128:(k + 1) * 128],
                                                                               identity=ident_b[:])) for k in range(8)],
                                         reads=[R_xb[s], R_ident], writes=[R_ptr[s]]))
                T.append(lambda: P.op("scalar", lambda e: e.activation(out=xT[s][:].rearrange("p a b -> p (a b)"), in_=ptr[s][:, :], func=AF.Copy),
                                      reads=[R_ptr[s]], writes=[R_xT[s]]))
                for nb, (c0, cw, pc) in enumerate(colblocks):
                    T.append(lambda nb=nb, c0=c0, cw=cw: P.group("tensor", [(lambda e, k=k: e.matmul(
                        pp[nb][:, 0:cw], lhsT=xT[s][:, k, :], rhs=w_in_bf[:, k, c0:c0 + cw], start=(k == 0), stop=(k == 7))) for k in range(8)],
                        reads=[R_xT[s], R_w], writes=[R_pp[nb]]))
                    if nb % 2 == 0:
                        T.append(lambda nb=nb, pc=pc, cw=cw: P.op("vector", lambda e: e.tensor_copy(out=pj[s][:, pc:pc + cw], in_=pp[nb][:, 0:cw]),
                                                                  reads=[R_pp[nb]], writes=[R_pj[s]]))
                    else:
                        T.append(lambda nb=nb, pc=pc, cw=cw: P.op("scalar", lambda e: e.activation(out=pj[s][:, pc:pc + cw], in_=pp[nb][:, 0:cw], func=AF.Copy),
                                                                  reads=[R_pp[nb]], writes=[R_pj[s]]))
                for half in range(2):
                    fns = []
                    for j in (range(0, 4) if half == 0 else range(4, 6)):
                        for k in range(8):
                            w_ap = w_in_bf[:, k, 1024 + j * 128:1024 + (j + 1) * 128] if j < 4 else w_kd[:, k, j - 4, :]
                            fns.append(lambda e, k=k, j=j, w_ap=w_ap: e.matmul(
                                pq[:, j * 128:(j + 1) * 128], lhsT=w_ap, rhs=xT[s][:, k, :], start=(k == 0), stop=(k == 7)))
                    T.append(lambda fns=fns, half=half: P.group("tensor", fns, reads=[R_xT[s], R_w], writes=[R_pq[half]]))
                    if half == 0:
                        T.append(lambda: P.op("vector", lambda e: e.tensor_copy(out=qk[s][:, 0:4, :].rearrange("p a b -> p (a b)"), in_=pq[:, 0:512]),
                                              reads=[R_pq[0]], writes=[R_qk[s]]))
                    else:
                        T.append(lambda: P.op("scalar", lambda e: e.activation(out=qk[s][:, 4:6, :].rearrange("p a b -> p (a b)"), in_=pq[:, 512:768],
                                                                               func=AF.Copy), reads=[R_pq[1]], writes=[R_qk[s]]))
                T.append(lambda: P.dma("gpsimd", lambda e: e.dma_start(out=proj[b * 128:(b + 1) * 128, :], in_=pj[s][:]),
                                       f"pj{s}", reads=[R_pj[s]], writes=[R_proj[b]]))
                dq = mkap(qkT, b * 128, [[NTOK, 128], [128 * NTOK, 6], [1, 128]])
                T.append(lambda: P.dma("gpsimd", lambda e: e.dma_start(out=dq, in_=qk[s][:]), f"qk{s}", reads=[R_qk[s]], writes=[R_qkT[b]]))
                return T

            def record1(body):
                saved = (P.op, P.group, P.dma)
                L = []
                P.op = lambda *a, **k: L.append((a[0], lambda: saved[0](*a, **k)))
                P.group = lambda *a, **k: L.append((a[0], lambda: saved[1](*a, **k)))
                P.dma = lambda *a, **k: L.append(("dma", lambda: saved[2](*a, **k)))
                return L, saved

            SETUP2, _saved = record1(None)
            s2a = ExitStack()
            sb2a = lambda n, s, d=F32: sb(n, s, d, s2a)
            BR = sb2a("BR", [128, 64, 16]); BI = sb2a("BI", [128, 64, 16])
            CR = sb2a("CR", [128, 64, 16]); CI = sb2a("CI", [128, 64, 16])
            TMPA = sb2a("TMPA", [128, 2560]); TMPB = sb2a("TMPB", [128, 2560]); TMPI = sb2a("TMPI", [128, 2560], I32)
            MAG = sb2a("MAG", [128, 2560])
            sm = [sb2a(f"sm{i}", [128, 64]) for i in range(8)]
            CRraw = sb2a("CRraw", [128, 8, 2, 64]); LRraw = sb2a("LRraw", [64, 2, 64])

            def ld(dst, src_ap, name, slow=False):
                P.dma("sync", lambda e: e.dma_start(out=dst, in_=src_ap, allow_slow_non_contiguous=slow), "t", writes=[R_T])

            for h in range(2):
                hs = slice(h * 64, (h + 1) * 64)
                ld(BR[hs, :, :], mkap(b_re, 0, [[16, 64], [1024, 64], [1, 16]]), f"t2{h}")
                ld(BI[hs, :, :], mkap(b_im, 0, [[16, 64], [1024, 64], [1, 16]]), f"t3{h}")
            ld(DT[:], bcast_rows(log_step, 64), "t6")
            ld(DSK[:], bcast_rows(d_skip, 512), "t7")
            ld(NG[:].rearrange("p a b -> p (a b)"), c_ng[:, :], "t8")
            ld(CG[:], c_cg[:, :], "t9")
            ld(MASKF[:], c_maskf[:, :], "t10")
            ld(MASKB[:], c_maskb[:, :], "t11")

            ptw0 = pq
            R_raw = Res()
            for src_t, dst_t, nm in ((c_re, CR, "a"), (c_im, CI, "b")):
                for cp in range(2):
                    P.dma("sync", lambda e, src_t=src_t, cp=cp: e.dma_start(
                        out=CRraw[:, :, cp, :], in_=mkap(src_t, 0, [[64, 128], [8192, 8], [1, 64]])), "tr", writes=[R_raw])
                P.group("tensor", [(lambda e, a=a: e.transpose(out=ptw0[:, a * 128:(a + 1) * 128],
                                                               in_=CRraw[:, a, :, :].rearrange("p a b -> p (a b)"),
                                                               identity=ident_f[:])) for a in range(8)],
                        reads=[R_raw, R_ident], writes=[R_pq[0], R_pq[1]])
                P.op("vector", lambda e, dst_t=dst_t: e.tensor_copy(out=dst_t[:].rearrange("p a b -> p (a b)"), in_=ptw0[:, :]),
                     reads=[R_pq[0], R_pq[1]], writes=[R_T])
            for src_t, dst_t, nm in ((lam_re, LR, "c"), (lam_im, LI, "d")):
                for cp in range(2):
                    P.dma("sync", lambda e, src_t=src_t, cp=cp: e.dma_start(
                        out=LRraw[:, cp, :], in_=mkap(src_t, 0, [[64, 64], [1, 64]])), "tr", writes=[R_raw])
                P.op("tensor", lambda e: e.transpose(out=ptw0[:, 0:64], in_=LRraw[:, :, :].rearrange("p a b -> p (a b)"),
                                                     identity=ident_f[0:64, 0:64]), reads=[R_raw, R_ident], writes=[R_pq[0], R_pq[1]])
                P.op("vector", lambda e, dst_t=dst_t: e.tensor_copy(out=dst_t[:], in_=ptw0[:, 0:64]), reads=[R_pq[0], R_pq[1]], writes=[R_T])

            VX = []

            def vop(fn, eng=V):
                return P.op(eng, fn, reads=[R_T], writes=[R_T] + VX)

            TMPS = [TMPB, TMPI]

            def range_sin(dst, src, n, shift):
                kf = TMPS[0][:, 0:n]
                ki = TMPS[1][:, 0:n]
                vop(lambda e: e.tensor_scalar(out=kf, in0=src, scalar1=float(1.0 / TWO_PI), scalar2=float(shift / TWO_PI), op0=ALU.mult, op1=ALU.add))
                vop(lambda e: e.tensor_copy(out=ki, in_=kf))
                vop(lambda e: e.tensor_copy(out=kf, in_=ki))
                vop(lambda e: e.scalar_tensor_tensor(out=dst, in0=kf, scalar=-TWO_PI, in1=src, op0=ALU.mult, op1=ALU.add))
                vop(lambda e: e.tensor_scalar(out=dst, in0=dst, scalar1=float(-3.14159 - shift), scalar2=float(3.14159 - shift), op0=ALU.max, op1=ALU.min))
                vop(lambda e: e.activation(out=dst, in_=dst, func=AF.Sin, bias=float(shift)), "scalar")

            vop(lambda e: e.activation(out=DT[:], in_=DT[:], func=AF.Exp), "scalar")
            vop(lambda e: e.tensor_tensor(out=AR[:], in0=LR[:], in1=DT[:], op=ALU.mult))
            vop(lambda e: e.tensor_tensor(out=TH[:], in0=LI[:], in1=DT[:], op=ALU.mult))
            vop(lambda e: e.tensor_scalar(out=TH32[:], in0=TH[:], scalar1=32.0, scalar2=None, op0=ALU.mult))
            vop(lambda e: e.activation(out=M32[:], in_=AR[:], func=AF.Exp, scale=32.0), "scalar")
            ngb = mkap(NG, NG[:].offset, [list(NG[:].ap[0]), [40, 2], [0, 32], [1, 40]])
            bc40 = lambda T: mkap(T, T[:].offset, [list(T[:].ap[0]), [32, 2], [1, 32], [0, 40]])
            v4 = lambda T: mkap(T, T[:].offset, [list(T[:].ap[0]), [1280, 2], [40, 32], [1, 40]])
            vop(lambda e: e.tensor_tensor(out=v4(MAG), in0=bc40(AR), in1=ngb, op=ALU.mult))
            vop(lambda e: e.activation(out=MAG[:], in_=MAG[:], func=AF.Exp), "scalar")
            vop(lambda e: e.tensor_tensor(out=v4(TMPA), in0=bc40(TH), in1=ngb, op=ALU.mult))
            PREf = PRE[:].rearrange("p a b -> p (a b)")
            PIMf = PIM[:].rearrange("p a b -> p (a b)")
            range_sin(PIMf, TMPA[:], 2560, 0.0)
            range_sin(PREf, TMPA[:], 2560, PI / 2)
            vop(lambda e: e.tensor_tensor(out=PREf, in0=PREf, in1=MAG[:], op=ALU.mult))
            vop(lambda e: e.tensor_tensor(out=PIMf, in0=PIMf, in1=MAG[:], op=ALU.mult))
            lbre, lbim, den, nr, cre, cim, t0_, t1_ = sm
            for dd, idx in ((0, 8), (1, 31)):
                gs = slice(dd * 32, (dd + 1) * 32)
                vop(lambda e, gs=gs, idx=idx: e.tensor_copy(out=lbre[:, gs], in_=PRE[:, gs, idx]))
                vop(lambda e, gs=gs, idx=idx: e.tensor_copy(out=lbim[:, gs], in_=PIM[:, gs, idx]))
            vop(lambda e: e.tensor_tensor(out=den[:], in0=LR[:], in1=LR[:], op=ALU.mult))
            vop(lambda e: e.tensor_tensor(out=t0_[:], in0=LI[:], in1=LI[:], op=ALU.mult))
            vop(lambda e: e.tensor_tensor(out=den[:], in0=den[:], in1=t0_[:], op=ALU.add))
            vop(lambda e: e.reciprocal(out=den[:], in_=den[:]))
            vop(lambda e: e.tensor_scalar(out=nr[:], in0=lbre[:], scalar1=-1.0, scalar2=None, op0=ALU.add))
            vop(lambda e: e.tensor_tensor(out=cre[:], in0=nr[:], in1=LR[:], op=ALU.mult))
            vop(lambda e: e.tensor_tensor(out=t0_[:], in0=lbim[:], in1=LI[:], op=ALU.mult))
            vop(lambda e: e.tensor_tensor(out=cre[:], in0=cre[:], in1=t0_[:], op=ALU.add))
            vop(lambda e: e.tensor_tensor(out=cre[:], in0=cre[:], in1=den[:], op=ALU.mult))
            vop(lambda e: e.tensor_tensor(out=cim[:], in0=lbim[:], in1=LR[:], op=ALU.mult))
            vop(lambda e: e.tensor_tensor(out=t0_[:], in0=nr[:], in1=LI[:], op=ALU.mult))
            vop(lambda e: e.tensor_tensor(out=cim[:], in0=cim[:], in1=t0_[:], op=ALU.subtract))
            vop(lambda e: e.tensor_tensor(out=cim[:], in0=cim[:], in1=den[:], op=ALU.mult))
            bc16 = lambda T: mkap(T, T[:].offset, [list(T[:].ap[0]), [1, 64], [0, 16]])
            BBre = mkap(TMPA, TMPA[:].offset, [list(TMPA[:].ap[0]), [16, 64], [1, 16]])
            BBim = mkap(TMPA, TMPA[:].offset + 1024, [list(TMPA[:].ap[0]), [16, 64], [1, 16]])
            Tm = mkap(TMPB, TMPB[:].offset, [list(TMPB[:].ap[0]), [16, 64], [1, 16]])
            vop(lambda e: e.tensor_tensor(out=BBre, in0=bc16(cre), in1=BR[:], op=ALU.mult))
            vop(lambda e: e.tensor_tensor(out=Tm, in0=bc16(cim), in1=BI[:], op=ALU.mult))
            vop(lambda e: e.tensor_tensor(out=BBre, in0=BBre, in1=Tm, op=ALU.subtract))
            vop(lambda e: e.tensor_tensor(out=BBim, in0=bc16(cre), in1=BI[:], op=ALU.mult))
            vop(lambda e: e.tensor_tensor(out=Tm, in0=bc16(cim), in1=BR[:], op=ALU.mult))
            vop(lambda e: e.tensor_tensor(out=BBim, in0=BBim, in1=Tm, op=ALU.add))

            def hv(T3, h):
                return T3[h * 64:(h + 1) * 64, :, :]

            def hva(apfull, h):
                base = TMPA[h * 64:(h + 1) * 64, :]
                return mkap(TMPA, base.offset + (apfull.offset - TMPA[:].offset), [list(base.ap[0]), [16, 64], [1, 16]])

            vop(lambda e: e.tensor_copy(out=hv(X1, 0), in_=hva(BBre, 0)))
            vop(lambda e: e.tensor_copy(out=hv(X1, 1), in_=hva(BBim, 1)))
            vop(lambda e: e.tensor_scalar(out=hv(X2, 0), in0=hva(BBim, 0), scalar1=-1.0, scalar2=None, op0=ALU.mult))
            vop(lambda e: e.tensor_copy(out=hv(X2, 1), in_=hva(BBre, 1)))
            vop(lambda e: e.tensor_copy(out=hv(Y1, 0), in_=hv(CR, 0)))
            vop(lambda e: e.tensor_scalar(out=hv(Y1, 1), in0=hv(CI, 1), scalar1=-1.0, scalar2=None, op0=ALU.mult))
            vop(lambda e: e.tensor_scalar(out=hv(Y2, 0), in0=hv(CI, 0), scalar1=-1.0, scalar2=None, op0=ALU.mult))
            vop(lambda e: e.tensor_scalar(out=hv(Y2, 1), in0=hv(CR, 1), scalar1=-1.0, scalar2=None, op0=ALU.mult))

            P.op, P.group, P.dma = _saved
            sched1 = []
            for b in range(NB):
                L = (s1_load(0) if b == 0 else []) + (s1_load(b + 1) if b + 1 < NB else []) + s1_block(b)
                n = len(L)
                for i, th in enumerate(L):
                    sched1.append((b * 0.5 + i / n, b, i, th))
            merged = []
            i = 0
            while i < len(SETUP2):
                if SETUP2[i][0] == "tensor" and i + 1 < len(SETUP2):
                    f1_, f2_ = SETUP2[i][1], SETUP2[i + 1][1]
                    merged.append(lambda f1_=f1_, f2_=f2_: (f1_(), f2_()))
                    i += 2
                else:
                    merged.append(SETUP2[i][1])
                    i += 1
            for i, th in enumerate(merged):
                sched1.append((0.26 + 12.0 * i / max(1, len(merged)), -1, i, th))
            sched1.sort(key=lambda t: (t[0], t[1], t[2]))
            for _, _, _, th in sched1:
                th()
            s2a.close()

        P.barrier()
        with ExitStack() as s2:
          if upto >= 2:
            sb2 = lambda n, s, d=F32: sb(n, s, d, s2)
            ps2 = lambda n, s, d=F32: ps(n, s, d, s2)
            V = "vector"
            T4all = sb2("T4all", [128, 2560]); T4pall = sb2("T4pall", [128, 2560])
            TMPG = T4all[:, 0:2048]
            TMPS[0] = T4pall[:, 0:1024]
            TMPS[1] = T4pall[:, 1024:2048].bitcast(I32)
            XG = sb2("XG", [128, NCT, 32, 128], BF16)
            XP = sb2("XP", [128, NCT, 4, 32, 16], BF16)
            R_XP = Res()
            UG = sb2("UG", [128, 4, 4, NC], BF16)
            YG = sb2("YG", [128, 4, 4, NC], BF16)
            COSR = sb2("COSR", [128, 8, 2, 128]); SINR = sb2("SINR", [128, 8, 2, 128])
            T4 = [T4all[:, i * 1280:(i + 1) * 1280].rearrange("p (a b) -> p a b", b=640) for i in range(2)]
            T4p = [T4pall[:, i * 1280:(i + 1) * 1280].rearrange("p (a b) -> p a b", b=640) for i in range(2)]
            R_T4p = Res()
            WP = []
            for wi in range(2):
                WP.append({"t": (sb2(f"EN{wi}", [128, 2, 512], BF16), sb2(f"ES{wi}", [128, 2, 512], BF16),
                                 sb2(f"HA{wi}", [128, 2, 640], BF16), sb2(f"HB{wi}", [128, 2, 640], BF16),
                                 sb2(f"WA{wi}", [128, 16, 128], BF16), sb2(f"TW{wi}", [128, 7, 128], BF16),
                                 sb2(f"m0{wi}", [128, 128]), sb2(f"m1{wi}", [128, 128])),
                           "r": (Res(), Res(), Res(), Res(), Res())})
            tA = sb2("tA", [128, NC]); tB = sb2("tB", [128, NC]); Rr = tA
            SBf = [sb2(f"SBf{d}", [128, PADC]) for d in range(2)]
            RCs = [[sb2(f"RC{p}{d}", [128, PADC], BF16) for d in range(2)] for p in range(2)]
            RSns = [[sb2(f"RSn{p}{d}", [128, PADC], BF16) for d in range(2)] for p in range(2)]
            R_rcs = [Res(), Res()]
            RC, RSn = RCs[0], RSns[0]
            dummy = sb2("dummy_t", [128, 1])
            ptr2 = ps2("ptr2", [128, 1024], BF16)
            ptw = ps2("ptw", [128, 512])
            R_ptw = Res()
            py = [ps2(f"py{i}", [128, 512]) for i in range(2)]
            R_py = [Res(), Res()]
            pz = [ps2(f"pz{i}", [128, 512]) for i in range(4)]
            R_UG, R_YG, R_rot = Res(), Res(), Res()
            R_XGp = [[Res() for _ in range(4)] for _ in range(NCT)]
            R_T4 = Res()
            R_ptr2 = Res()
            R_pz = [Res() for _ in range(4)]
            R_st = Res()
            R_rc = R_rcs[0]
            for d in range(2):
                P.op(V, lambda e, d=d: e.memset(SBf[d][:], 0.0), writes=[R_st])
                for p_ in range(2):
                    P.op(V, lambda e, d=d, p_=p_: e.memset(RCs[p_][d][:], 0.0), writes=[R_rcs[p_]])
                    P.op(V, lambda e, d=d, p_=p_: e.memset(RSns[p_][d][:], 0.0), writes=[R_rcs[p_]])

            def seg_views():
                out = []
                sidx = 0
                for (c0, ns, ncs) in segs:
                    out.append((c0, ns, ncs, c0 + sidx))
                    sidx += ns
                return out
            SEGS = seg_views()

            def grp(part, g, g4, g8, W):
                EN, ES, HA, HB, WA, TW, m0, m1 = W["t"]
                R_EN, R_HA, R_WA, R_TW, R_m = W["r"]
                if part == "w":
                    def pw_e(T):
                        return mkap(T, T[:].offset + g * 40 + 38, [list(T[:].ap[0]), [1274, 2], [-1, 32], [0, 16]])

                    def xv_e(T):
                        return mkap(T, T[:].offset + g * 16, [list(T[:].ap[0]), [512, 2], [0, 32], [1, 16]])

                    def pw_h(T):
                        return mkap(T, T[:].offset + g * 40, [list(T[:].ap[0]), [1280, 2], [1, 40], [0, 16]])

                    def xv_h(T):
                        return mkap(T, T[:].offset + g * 16, [list(T[:].ap[0]), [512, 2], [0, 40], [1, 16]])

                    def t4e(i):
                        return mkap(T4[i], T4[i][:].offset, [list(T4[i][:].ap[0]), [640, 2], [16, 32], [1, 16]])

                    def t4h(i):
                        return mkap(T4[i], T4[i][:].offset, [list(T4[i][:].ap[0]), [640, 2], [16, 40], [1, 16]])

                    def e3(T, n):
                        return mkap(T, T[:].offset, [list(T[:].ap[0]), [T[:].ap[1][0], 2], [16, n], [1, 16]])

                    rw = dict(reads=[R_T, R_EN], writes=[R_T4])
                    P.op(V, lambda e, a=pw_e(PRE), b=xv_e(X1): e.tensor_tensor(out=t4e(0), in0=a, in1=b, op=ALU.mult), **rw)
                    P.op(V, lambda e, a=pw_e(PIM), b=xv_e(X2): e.tensor_tensor(out=t4e(1), in0=a, in1=b, op=ALU.mult), **rw)
                    P.op(V, lambda e: e.tensor_tensor(out=e3(EN, 32), in0=t4e(0), in1=t4e(1), op=ALU.add),
                         reads=[R_T4], writes=[R_EN])
                    P.op(V, lambda e, a=pw_e(PRE), b=xv_e(X2): e.tensor_tensor(out=t4e(0), in0=a, in1=b, op=ALU.mult), **rw)
                    P.op(V, lambda e, a=pw_e(PIM), b=xv_e(X1): e.tensor_tensor(out=t4e(1), in0=a, in1=b, op=ALU.mult), **rw)
                    P.op(V, lambda e: e.tensor_tensor(out=ES[:, 0, :], in0=T4[1][:, 0, 0:512], in1=T4[0][:, 0, 0:512],
                                                      op=ALU.subtract), reads=[R_T4], writes=[R_EN])
                    P.op(V, lambda e: e.tensor_tensor(out=ES[:, 1, :], in0=T4[0][:, 1, 0:512], in1=T4[1][:, 1, 0:512],
                                                      op=ALU.subtract), reads=[R_T4], writes=[R_EN])
                    MARK("H")
                    def t4hp(i):
                        return mkap(T4p[i], T4p[i][:].offset, [list(T4p[i][:].ap[0]), [640, 2], [16, 40], [1, 16]])
                    G = "gpsimd"
                    rwp = dict(reads=[R_T, R_HA], writes=[R_T4p])
                    P.op(G, lambda e, a=pw_h(PRE), b=xv_h(Y1): e.tensor_tensor(out=t4hp(0), in0=a, in1=b, op=ALU.mult), **rwp)
                    P.op(G, lambda e, a=pw_h(PIM), b=xv_h(Y2): e.tensor_tensor(out=t4hp(1), in0=a, in1=b, op=ALU.mult), **rwp)
                    P.op(G, lambda e: e.tensor_tensor(out=HA[:], in0=T4p[0][:], in1=T4p[1][:], op=ALU.add),
                         reads=[R_T4p], writes=[R_HA])
                    P.op(G, lambda e, a=pw_h(PRE), b=xv_h(Y2): e.tensor_tensor(out=t4hp(0), in0=a, in1=b, op=ALU.mult), **rwp)
                    P.op(G, lambda e, a=pw_h(PIM), b=xv_h(Y1): e.tensor_tensor(out=t4hp(1), in0=a, in1=b, op=ALU.mult), **rwp)
                    P.op(G, lambda e: e.tensor_tensor(out=HB[:, 0, :], in0=T4p[0][:, 0, :], in1=T4p[1][:, 0, :],
                                                      op=ALU.subtract), reads=[R_T4p], writes=[R_HA])
                    P.op(G, lambda e: e.tensor_tensor(out=HB[:, 1, :], in0=T4p[1][:, 1, :], in1=T4p[0][:, 1, :],
                                                      op=ALU.subtract), reads=[R_T4p], writes=[R_HA])
                    MARK("W")
                    for ns, Tsrc in ((0, EN), (1, ES)):
                        fns = []
                        for d in range(2):
                            for jb in range(4):
                                k = d * 4 + jb
                                fns.append(lambda e, Tsrc=Tsrc, d=d, jb=jb, k=k: e.transpose(
                                    out=ptr2[:, k * 128:(k + 1) * 128], in_=Tsrc[:, d, jb * 128:(jb + 1) * 128],
                                    identity=ident_b[:]))
                        P.group("tensor", fns, reads=[R_EN, R_ident], writes=[R_ptr2])
                        P.op("scalar", lambda e, ns=ns: e.activation(
                            out=WA[:, ns * 8:(ns + 1) * 8, :].rearrange("p a b -> p (a b)"), in_=ptr2[:, :], func=AF.Copy),
                             reads=[R_ptr2], writes=[R_WA])
                    MARK("T")
                    fns = []
                    for dl in range(4):
                        fns.append(lambda e, dl=dl: e.matmul(ptw[:, dl * 128:(dl + 1) * 128], lhsT=EN[:, 0, 384:512],
                                                             rhs=HA[:, 0, dl * 128:(dl + 1) * 128], start=True, stop=True))
                    P.group("tensor", fns, reads=[R_EN, R_HA], writes=[R_ptw])
                    P.op(V, lambda e: e.tensor_copy(out=TW[:, 4:7, :].rearrange("p a b -> p (a b)"), in_=ptw[:, 128:512]),
                         reads=[R_ptw], writes=[R_TW])
                    P.op(V, lambda e: e.tensor_tensor(out=m0[:], in0=ptw[:, 0:128], in1=MASKF[:], op=ALU.mult),
                         reads=[R_ptw, R_T], writes=[R_m])
                    fns = []
                    for dl in range(4):
                        fns.append(lambda e, dl=dl: e.matmul(ptw[:, dl * 128:(dl + 1) * 128], lhsT=EN[:, 1, 0:128],
                                                             rhs=HA[:, 1, (32 - 8 * dl) * 16:(40 - 8 * dl) * 16],
                                                             start=True, stop=True))
                    P.group("tensor", fns, reads=[R_EN, R_HA], writes=[R_ptw])
                    for dl in range(1, 4):
                        P.op(V, lambda e, dl=dl: e.tensor_copy(out=TW[:, 3 - dl, :], in_=ptw[:, dl * 128:(dl + 1) * 128]),
                             reads=[R_ptw], writes=[R_TW])
                    P.op(V, lambda e: e.tensor_tensor(out=m1[:], in0=ptw[:, 0:128], in1=MASKB[:], op=ALU.mult),
                         reads=[R_ptw, R_T], writes=[R_m])
                    P.op(V, lambda e: e.tensor_tensor(out=m0[:], in0=m0[:], in1=m1[:], op=ALU.add), reads=[R_m], writes=[R_m])
                    dskv = mkap(DSK, DSK[:].offset + g * 16, [list(DSK[:].ap[0]), [0, 8], [1, 16]])
                    P.op(V, lambda e, dskv=dskv: e.tensor_tensor(out=m1[:].rearrange("p (a b) -> p a b", b=16),
                                                                 in0=ident_f[:].rearrange("p (a b) -> p a b", b=16),
                                                                 in1=dskv, op=ALU.mult), reads=[R_T, R_ident], writes=[R_m])
                    P.op(V, lambda e: e.tensor_tensor(out=TW[:, 3, :], in0=m0[:], in1=m1[:], op=ALU.add),
                         reads=[R_m], writes=[R_TW])
                else:
                    for d in range(2):
                        for ns in range(2):
                            P.group("tensor", [(lambda e, d=d, ns=ns, jb=jb, g4=g4: e.matmul(
                                pz[d * 2 + ns][:, 0:NC], lhsT=WA[:, ns * 8 + d * 4 + jb, :], rhs=UG[:, g4, jb, :],
                                start=(jb == 0), stop=(jb == 3))) for jb in range(4)],
                                    reads=[R_WA, R_UG], writes=[R_pz[d * 2 + ns]])
                    for d in range(2):
                        for (c0, nsq, ncs, pc0) in SEGS:
                            def zv(T, c0=c0, nsq=nsq, ncs=ncs):
                                return sub(T[:, c0:c0 + nsq * ncs], 0, [[ncs, nsq], [1, ncs]])

                            def tbv(T, g8=g8, d=d, nsq=nsq, ncs=ncs):
                                return sub(T[:, g8, d, 0:ncs], 0, [[0, nsq], [1, ncs]])
                            P.op(V, lambda e, o=zv(tA), i0=zv(pz[d * 2]), i1=tbv(COSR): e.tensor_tensor(out=o, in0=i0, in1=i1, op=ALU.mult),
                                 reads=[R_pz[d * 2], R_rot], writes=[R_st])
                            P.op(V, lambda e, o=zv(tB), i0=zv(pz[d * 2 + 1]), i1=tbv(SINR): e.tensor_tensor(out=o, in0=i0, in1=i1, op=ALU.mult),
                                 reads=[R_pz[d * 2 + 1], R_rot], writes=[R_st])
                            P.op(V, lambda e, o=zv(Rr), i0=zv(tA), i1=zv(tB): e.tensor_tensor(out=o, in0=i0, in1=i1, op=ALU.add),
                                 reads=[R_st], writes=[R_st])
                            dec = sub(M32[:, d * 32 + g:d * 32 + g + 1], 0, [[0, ncs]])
                            for sq in range(nsq):
                                pcol = pc0 + sq * (ncs + 1)
                                ccol = c0 + sq * ncs
                                if d == 0:
                                    o_ap = SBf[0][:, pcol + 1:pcol + 1 + ncs]
                                    i_ap = Rr[:, ccol:ccol + ncs]
                                else:
                                    o_ap = sub(SBf[1][:, pcol + ncs - 1:pcol + ncs], 0, [[-1, ncs]])
                                    i_ap = sub(Rr[:, ccol + ncs - 1:ccol + ncs], 0, [[-1, ncs]])
                                P.op(V, lambda e, o_ap=o_ap, i_ap=i_ap, dec=dec: e.tensor_tensor_scan(
                                    out=o_ap, data0=dec, data1=i_ap, initial=0.0, op0=ALU.mult, op1=ALU.add),
                                     reads=[R_st, R_T], writes=[R_st])
                            off = 1 if d == 0 else 0

                            def sv(T, pc0=pc0, off=off, ncs=ncs, nsq=nsq):
                                return sub(T[:, pc0 + off:pc0 + off + 1], 0, [[ncs + 1, nsq], [1, ncs]])
                            P.op(V, lambda e, o=sv(RC[d]), i0=sv(SBf[d]), i1=tbv(COSR): e.tensor_tensor(out=o, in0=i0, in1=i1, op=ALU.mult),
                                 reads=[R_st, R_rot], writes=[R_rc])
                            P.op(V, lambda e, o=sv(RSn[d]), i0=sv(SBf[d]), i1=tbv(SINR): e.tensor_tensor(out=o, in0=i0, in1=i1, op=ALU.mult),
                                 reads=[R_st, R_rot], writes=[R_rc])
                    MARK("R")
                    for tb_ in range(4):
                        fns = []
                        for (c0, nsq, ncs, pc0) in SEGS:
                            for sq in range(nsq):
                                cc = c0 + sq * ncs
                                osl = py[tb_ % 2][:, cc:cc + ncs]
                                for jb in range(4):
                                    fns.append(lambda e, osl=osl, w=TW[:, tb_ - jb + 3, :], r=UG[:, g4, jb, cc:cc + ncs], jb=jb: e.matmul(
                                        osl, lhsT=w, rhs=r, start=(jb == 0), stop=False))
                                for d in range(2):
                                    off = 0 if d == 0 else 1
                                    if d == 0:
                                        wa = HA[:, 0, (8 * tb_ + 8) * 16:(8 * tb_ + 16) * 16]
                                        wb = HB[:, 0, (8 * tb_ + 8) * 16:(8 * tb_ + 16) * 16]
                                    else:
                                        wa = HA[:, 1, (8 * tb_) * 16:(8 * tb_ + 8) * 16]
                                        wb = HB[:, 1, (8 * tb_) * 16:(8 * tb_ + 8) * 16]
                                    pcol = pc0 + sq * (ncs + 1) + off
                                    last = (d == 1)
                                    fns.append(lambda e, osl=osl, wa=wa, r=RC[d][:, pcol:pcol + ncs]: e.matmul(
                                        osl, lhsT=wa, rhs=r, start=False, stop=False))
                                    fns.append(lambda e, osl=osl, wb=wb, r=RSn[d][:, pcol:pcol + ncs], last=last: e.matmul(
                                        osl, lhsT=wb, rhs=r, start=False, stop=last))
                        P.group("tensor", fns, reads=[R_TW, R_UG, R_HA, R_rc], writes=[R_py[tb_ % 2]])
                        P.op("scalar", lambda e, tb_=tb_, g4=g4: e.activation(out=YG[:, g4, tb_, :], in_=py[tb_ % 2][:, 0:NC], func=AF.Copy),
                             reads=[R_py[tb_ % 2]], writes=[R_YG])

            REC = [None]

            def MARK(name):
                if REC[0] is not None:
                    REC[0].append(("mark", name))

            def record(body):
                saved_op, saved_group = P.op, P.group
                L = []
                REC[0] = L
                P.op = lambda *a, **k: L.append(("th", lambda: saved_op(*a, **k)))
                P.group = lambda *a, **k: L.append(("th", lambda: saved_group(*a, **k)))
                try:
                    body()
                finally:
                    P.op, P.group = saved_op, saved_group
                    REC[0] = None
                secs = {"": []}
                cur = ""
                for kind, v in L:
                    if kind == "mark":
                        cur = v
                        secs.setdefault(cur, [])
                    else:
                        secs[cur].append(v)
                return secs

            def emit_interleaved(*lists):
                items = []
                for li, L in enumerate(lists):
                    for i, th in enumerate(L):
                        items.append(((i + 0.5) / len(L), li, i, th))
                items.sort(key=lambda t: (t[0], t[1], t[2]))
                for _, _, _, th in items:
                    th()

            def rec_w(g):
                sec = record(lambda: grp("w", g, 0, 0, WP[g % 2]))
                return sec[""] + sec["W"], sec["H"] + sec["T"]

            def rec_s(g, g4, g8):
                nonlocal RC, RSn, R_rc
                RC, RSn, R_rc = RCs[g % 2], RSns[g % 2], R_rcs[g % 2]
                sec = record(lambda: grp("s", g, g4, g8, WP[g % 2]))
                return sec[""], sec["R"]

            wE0, wH0 = rec_w(0)
            for th in wE0 + wH0:
                th()

            for gb in range(4):
                for ct in range(NCT):
                    for tq in range(4):
                        src = mkap(proj, (128 * ct * 32 + tq * 8) * PJ + gb * 128, [[32 * PJ, 128], [PJ, 8], [1, 128]])
                        tk = P.dma("sync", lambda e, src=src, ct=ct, tq=tq: e.dma_start(out=XG[:, ct, tq * 8:(tq + 1) * 8, :], in_=src),
                                   f"xg{ct}", reads=R_proj, writes=[R_XGp[ct][tq]])
                    for tq in range(4):
                        R_XGp[ct][tq].w = tk
                thv = mkap(TH32, TH32[:].offset + gb * 8, [list(TH32[:].ap[0]), [1, 8], [32, 2], [0, 128]])
                cgv = mkap(CG, CG[:].offset, [list(CG[:].ap[0]), [0, 8], [0, 2], [1, 128]])
                ta4 = mkap(TMPG, TMPG[:].offset, [list(TMPG[:].ap[0]), [256, 8], [128, 2], [1, 128]])
                VX[:] = [R_T4, R_T4p]
                P.op(V, lambda e, thv=thv, cgv=cgv, ta4=ta4: e.tensor_tensor(out=ta4, in0=thv, in1=cgv, op=ALU.mult),
                     reads=[R_T], writes=[R_T, R_T4, R_T4p])
                P.op(V, lambda e: e.memset(dummy[:], 0.0), reads=[R_T], writes=[R_rot, R_T])
                for hh in range(2):
                    range_sin(SINR[:].rearrange("p a b c -> p (a b c)")[:, hh * 1024:(hh + 1) * 1024], TMPG[:, hh * 1024:(hh + 1) * 1024], 1024, 0.0)
                    range_sin(COSR[:].rearrange("p a b c -> p (a b c)")[:, hh * 1024:(hh + 1) * 1024], TMPG[:, hh * 1024:(hh + 1) * 1024], 1024, PI / 2)
                P.op(V, lambda e: e.memset(dummy[:], 0.0), reads=[R_T], writes=[R_rot, R_T])
                VX[:] = []
                for hb in range(2):
                    for ct in range(NCT):
                        xin = sub(XG[:, ct, 0, hb * 64:hb * 64 + 1], 0, [[16, 4], [128, 32], [1, 16]])
                        P.op(V, lambda e, xin=xin, ct=ct: e.tensor_copy(out=XP[:, ct, :, :, :], in_=xin),
                             reads=R_XGp[ct], writes=[R_XP])
                    for ct in range(NCT):
                        for g2 in range(0, 4, 2):
                            fns = []
                            for gi in range(2):
                                for jb in range(4):
                                    src_ap = XP[:, ct, g2 + gi, 8 * jb:8 * jb + 8, :].rearrange("p a b -> p (a b)")
                                    k = gi * 4 + jb
                                    fns.append(lambda e, src_ap=src_ap, k=k: e.transpose(
                                        out=ptr2[:, k * 128:(k + 1) * 128], in_=src_ap, identity=ident_b[:]))
                            P.group("tensor", fns, reads=[R_XP, R_ident], writes=[R_ptr2])
                            dst = sub(UG[:, g2, 0, ct * 128:(ct + 1) * 128], 0, [[NC, 8], [1, 128]])
                            P.op("scalar", lambda e, dst=dst: e.activation(
                                out=dst, in_=ptr2[:, :].rearrange("p (a b) -> p a b", b=128), func=AF.Copy),
                                 reads=[R_ptr2], writes=[R_UG])
                    for g4 in range(4):
                        g8 = hb * 4 + g4
                        g = gb * 8 + g8
                        sA, sB = rec_s(g, g4, g8)
                        wE, wH = rec_w(g + 1) if g + 1 < 32 else ([], [])
                        if g4 == 0:
                            emit_interleaved(sA, wE)
                        else:
                            emit_interleaved(prev_sB, sA, wE)
                        for th in wH:
                            th()
                        prev_sB = sB
                    for th in prev_sB:
                        th()
                    for ct in range(NCT):
                        for g2 in range(0, 4, 2):
                            fns = []
                            for gi in range(2):
                                for tb_ in range(4):
                                    k = gi * 4 + tb_
                                    fns.append(lambda e, gi=gi, tb_=tb_, k=k, ct=ct, g2=g2: e.transpose(
                                        out=ptr2[:, k * 128:(k + 1) * 128], in_=YG[:, g2 + gi, tb_, ct * 128:(ct + 1) * 128],
                                        identity=ident_b[:]))
                            P.group("tensor", fns, reads=[R_YG, R_ident], writes=[R_ptr2])
                            for gi in range(2):
                                g8 = hb * 4 + g2 + gi
                                dst = sub(XG[:, ct, 0, 16 * g8:16 * g8 + 1], 0, [[1024, 4], [128, 8], [1, 16]])
                                srcp = sub(ptr2[:, gi * 512:gi * 512 + 1], 0, [[128, 4], [16, 8], [1, 16]])
                                P.op("scalar", lambda e, dst=dst, srcp=srcp: e.activation(out=dst, in_=srcp, func=AF.Gelu_apprx_tanh),
                                     reads=[R_ptr2], writes=R_XGp[ct])
                for ct in range(NCT):
                    for tq in range(4):
                        dstd = mkap(gd, (128 * ct * 32 + tq * 8) * 512 + gb * 128, [[32 * 512, 128], [512, 8], [1, 128]])
                        tk = P.dma("gpsimd", lambda e, dstd=dstd, ct=ct, tq=tq: e.dma_start(out=dstd, in_=XG[:, ct, tq * 8:(tq + 1) * 8, :]),
                                   f"gs{ct}", reads=[R_XGp[ct][tq]], writes=[R_gdp[gb][ct][tq]])
                    for tq in range(4):
                        R_gdp[gb][ct][tq].w = tk
                        for tq2 in range(4):
                            R_XGp[ct][tq2].r.append(tk)

        sT.close()
        P.barrier()
        with ExitStack() as s3:
          if upto >= 2.5:
            sb3 = lambda n, s, d=F32: sb(n, s, d, s3)
            ps3 = lambda n, s, d=F32: ps(n, s, d, s3)
            V = "vector"
            R_c = Res()
            w_out_bf = sb3("w_out_bf", [128, 8, D], BF16)
            w_glu_bf = sb3("w_glu_bf", [128, 4, 512], BF16)
            bglu = sb3("bglu", [128, 512]); nws = sb3("nws", [128, 512]); nwa = sb3("nwa", [128, 512])
            fnw = sb3("fnw", [128, D]); esink = sb3("esink", [128, 8])
            expbT = sb3("expbT", [128, 2, 3, 4, 128], BF16)
            sinkrow = sb3("sinkrow", [1, 8, 65], BF16); onesrow = sb3("onesrow", [1, 128], BF16)
            s3a = ExitStack()
            sb3a = lambda n, s, d=F32: sb(n, s, d, s3a)
            wst = sb3a("wst", [128, D])
            oh_f = sb3a("oh_f", [32, 512]); oh_b = sb3a("oh_b", [32, 512], BF16)
            rb_f = sb3a("rb_f", [32, 8]); rb_b = sb3a("rb_b", [32, 8], BF16)
            vld = sb3a("vld", [8, 512]); wv = sb3a("wv", [8, 512])
            exch_f = sb3a("exch_f", [128, 128]); exch_b = sb3a("exch_b", [128, 128], BF16)
            t2f = sb3a("t2f", [128, 8, 128]); t2b = sb3a("t2b", [128, 8, 128], BF16)
            R_wst = Res()
            for kc in range(8):
                P.dma("sync", lambda e, kc=kc: e.dma_start(out=wst[:], in_=w_out[kc * 128:(kc + 1) * 128, :]), "wst", writes=[R_wst])
                P.op(V, lambda e, kc=kc: e.tensor_copy(out=w_out_bf[:, kc, :], in_=wst[:]), reads=[R_wst], writes=[R_c])
            for kc in range(4):
                P.dma("sync", lambda e, kc=kc: e.dma_start(out=wst[:, 0:512], in_=w_glu[kc * 128:(kc + 1) * 128, :]), "wst", writes=[R_wst])
                P.op(V, lambda e, kc=kc: e.tensor_copy(out=w_glu_bf[:, kc, :], in_=wst[:, 0:512]), reads=[R_wst], writes=[R_c])
            for dst, src, n, nm in ((bglu, b_glu, 512, "k0"), (nws, ssm_norm_w, 512, "k1"), (nwa, attn_norm_w, 512, "k2"),
                                    (fnw, final_norm_w, D, "k3"), (esink, sink, 8, "k4")):
                P.dma("sync", lambda e, dst=dst, src=src, n=n: e.dma_start(out=dst[:], in_=bcast_rows(src, n)), "k", writes=[R_c])
            P.op("scalar", lambda e: e.activation(out=esink[:], in_=esink[:], func=AF.Exp), reads=[R_c], writes=[R_c])
            bank = [ps3(f"bank{i}", [128, 512]) for i in range(8)]
            pA = bank[0]
            R_pA = Res()
            R_wd = Res()
            P.dma("sync", lambda e: e.dma_start(out=oh_f[:], in_=c_oh[:, :]), "k", writes=[R_c])
            P.dma("sync", lambda e: e.dma_start(out=rb_f[:], in_=rel_bias[:, :]), "k", writes=[R_c])
            P.dma("sync", lambda e: e.dma_start(out=vld[:], in_=c_valid[:, :]), "k", writes=[R_c])
            P.dma("sync", lambda e: e.dma_start(out=exch_f[:], in_=c_exch[:, :]), "k", writes=[R_c])
            P.op(V, lambda e: e.tensor_copy(out=oh_b[:], in_=oh_f[:]), reads=[R_c], writes=[R_c])
            P.op(V, lambda e: e.tensor_copy(out=rb_b[:], in_=rb_f[:]), reads=[R_c], writes=[R_c])
            P.op(V, lambda e: e.tensor_copy(out=exch_b[:], in_=exch_f[:]), reads=[R_c], writes=[R_c])
            P.op("tensor", lambda e: e.matmul(pA[0:8, :], lhsT=rb_b[:], rhs=oh_b[:], start=True, stop=True), reads=[R_c], writes=[R_pA])
            P.op("scalar", lambda e: e.activation(out=wv[:], in_=pA[0:8, :], func=AF.Copy, scale=8.0), reads=[R_pA], writes=[R_c])
            P.op(V, lambda e: e.tensor_tensor(out=wv[:], in0=wv[:], in1=vld[:], op=ALU.mult), reads=[R_c], writes=[R_c])
            P.op(V, lambda e: e.tensor_scalar(out=vld[:], in0=vld[:], scalar1=240000.0, scalar2=-240000.0, op0=ALU.mult, op1=ALU.add),
                 reads=[R_c], writes=[R_c])
            P.op(V, lambda e: e.tensor_tensor(out=wv[:], in0=wv[:], in1=vld[:], op=ALU.add), reads=[R_c], writes=[R_c])
            P.op(V, lambda e: e.memset(sinkrow[:], 0.0), reads=[R_c], writes=[R_c])
            P.op(V, lambda e: e.memset(onesrow[:], 1.0), reads=[R_c], writes=[R_c])
            P.op(V, lambda e: e.tensor_copy(out=sub(sinkrow[0:1, 0, 64:65], 0, [[65, 8]]), in_=esink[0:1, :]), reads=[R_c], writes=[R_c])
            P.dma("sync", lambda e: e.dma_start(out=wd[:, :], in_=wv[:]), "k", reads=[R_c], writes=[R_wd, R_c])
            for rel in range(3):
                srcw = mkap(wd, (2 - rel) * 128, [[1, 128], [512, 8], [1, 128]])
                P.dma("sync", lambda e, srcw=srcw: e.dma_start(out=t2f[:], in_=srcw), "k", reads=[R_wd], writes=[R_c])
                P.op(V, lambda e: e.tensor_copy(out=t2b[:], in_=t2f[:]), reads=[R_c], writes=[R_c])
                for hh in range(2):
                    P.op("tensor", lambda e, hh=hh: e.matmul(pA[:, :], lhsT=exch_b[:], rhs=t2b[:, hh * 4:(hh + 1) * 4, :].rearrange("p a b -> p (a b)"),
                                                            start=True, stop=True), reads=[R_c], writes=[R_pA])
                    P.op(V, lambda e, hh=hh, rel=rel: e.tensor_copy(out=expbT[:, hh, rel, :, :],
                                                                   in_=pA[:, :].rearrange("p (a b) -> p a b", b=128)),
                         reads=[R_pA], writes=[R_c])
            s3a.close()
            P.barrier()
            NL = 6
            NS = 4
            gin = [sb3(f"gin{i}", [128, 512], BF16) for i in range(NL)]
            pin = [sb3(f"pin{i}", [128, 1024], BF16) for i in range(NL)]
            v3 = [sb3(f"v3{i}", [128, 3, 128], BF16) for i in range(NL)]
            qT = [sb3(f"qT{i}", [128, 4, 2, 128], BF16) for i in range(NL)]
            kT = [sb3(f"kT{i}", [128, 2, 384], BF16) for i in range(NL)]
            xr = [sb3(f"xr{i}", [128, D]) for i in range(NL)]
            R_gin, R_pin, R_v3, R_qT, R_kT, R_xr = ([Res() for _ in range(NL)] for _ in range(6))
            gT = [sb3(f"gT{i}", [128, 4, 128], BF16) for i in range(NS)]
            vaug = [sb3(f"vaug{i}", [128, 3, 2, 80], BF16) for i in range(NS)]
            Et = [[sb3(f"Et{i}_{j}", [128, 3, 512], BF16) for j in range(2)] for i in range(NS)]
            f1 = [sb3(f"f1{i}", [128, 512]) for i in range(NS)]
            ya = [sb3(f"ya{i}", [128, 512]) for i in range(NS)]
            szs = [sb3(f"szs{i}", [128, 1024]) for i in range(NS)]
            mixed = [sb3(f"mixed{i}", [128, D], BF16) for i in range(NS)]
            mT = [sb3(f"mT{i}", [128, 8, 128], BF16) for i in range(NS)]
            rr = [sb3(f"rr{i}", [128, D]) for i in range(NS)]
            junk3_ = sb3("junk3", [128, D], BF16)
            junk3 = [junk3_] * NS
            st3 = [sb3(f"st3{i}", [128, 32]) for i in range(NS)]
            RS = lambda: [Res() for _ in range(NS)]
            R_gT, R_vaug, R_f1, R_ya, R_szs, R_mixed, R_mT, R_rr, R_junk3, R_st3 = (RS() for _ in range(10))
            R_junk3 = [R_junk3[0]] * NS
            R_Et = [[Res(), Res()] for _ in range(NS)]
            R_X, R_Y = RS(), RS()
            for i in range(NS):
                P.op(V, lambda e, i=i: e.memset(vaug[i][:], 1.0), writes=[R_vaug[i]])
            for i in range(NL):
                P.op(V, lambda e, i=i: e.memset(qT[i][:], 0.0), writes=[R_qT[i]])
            st_toks = []

            def nbrs(b):
                lo = b - 1 if (b > 0 and blk_seq[b - 1] == blk_seq[b]) else b
                hi = b + 1 if (b + 1 < NB and blk_seq[b + 1] == blk_seq[b]) else b
                return lo, hi

            def loads(b):
                l = b % NL
                rows = slice(b * 128, (b + 1) * 128)
                lo, hi = nbrs(b)
                rlo, rhi = lo - b + 1, hi - b + 2
                nr_ = rhi - rlo
                ct = b // 32
                nm = f"ld{l}"

                def go():
                    P.dma("sync", lambda e: e.dma_start(out=gin[l][:], in_=gd[rows, :]), nm,
                          reads=[R_gdp[gbx][ct][tqx] for gbx in range(4) for tqx in range(4)], writes=[R_gin[l]])
                    P.dma("sync", lambda e: e.dma_start(out=pin[l][:, 0:512], in_=proj[rows, 512:1024]), nm,
                          reads=[R_proj[b]], writes=[R_pin[l]])
                    P.dma("sync", lambda e: e.dma_start(out=pin[l][:, 512:1024], in_=proj[rows, 1152:1664]), nm,
                          reads=[R_proj[b]])
                    srcv = mkap(proj, lo * 128 * PJ + 1024, [[PJ, 128], [128 * PJ, nr_], [1, 128]])
                    P.dma("sync", lambda e: e.dma_start(out=v3[l][:, rlo:rhi, :], in_=srcv), nm,
                          reads=[R_proj[j] for j in range(lo, hi + 1)], writes=[R_v3[l]])
                    P.wait_all("sync", [R_qT[l].w] + R_qT[l].r)
                    for hf in range(2):
                        srcq = mkap(qkT, hf * 64 * NTOK + b * 128, [[NTOK, 64], [128 * NTOK, 4], [1, 128]])
                        P.dma("sync", lambda e, srcq=srcq, hf=hf: e.dma_start(
                            out=qT[l][hf * 64:(hf + 1) * 64, :, hf, :], in_=srcq), nm, reads=[R_qkT[b]])
                    srck = mkap(qkT, 512 * NTOK + lo * 128, [[NTOK, 128], [128 * NTOK, 2], [1, nr_ * 128]])
                    P.dma("sync", lambda e: e.dma_start(out=kT[l][:, :, rlo * 128:rhi * 128], in_=srck), nm,
                          reads=[R_qkT[j] for j in range(lo, hi + 1)], writes=[R_kT[l]])
                    t = P.dma("sync", lambda e: e.dma_start(out=xr[l][:], in_=x[rows, :]), nm, writes=[R_xr[l]])
                    for R in (R_gin[l], R_pin[l], R_v3[l], R_qT[l], R_kT[l], R_xr[l]):
                        R.w = t
                        R.r = []
                return [go]

            def rstd_ops(T, s, col, n):
                T.append(lambda: P.op("scalar", lambda e: e.activation(out=st3[s][:, col + 1:col + 2], in_=st3[s][:, col:col + 1],
                                                                       func=AF.Ln, scale=1.0 / n, bias=1e-6),
                                      reads=[R_st3[s]], writes=[R_st3[s]]))
                T.append(lambda: P.op("scalar", lambda e: e.activation(out=st3[s][:, col + 1:col + 2], in_=st3[s][:, col + 1:col + 2],
                                                                       func=AF.Exp, scale=-0.5),
                                      reads=[R_st3[s]], writes=[R_st3[s]]))

            def sigmoid_ops(T, ap, rd, wr):
                T.append(lambda: P.op("scalar", lambda e: e.activation(out=ap, in_=ap, func=AF.Exp, scale=-1.0), reads=rd, writes=wr))
                T.append(lambda: P.op("scalar", lambda e: e.activation(out=ap, in_=ap, func=AF.Ln, bias=1.0), reads=wr, writes=wr))
                T.append(lambda: P.op("scalar", lambda e: e.activation(out=ap, in_=ap, func=AF.Exp, scale=-1.0), reads=wr, writes=wr))

            def compute(b):
                T = []
                s = b % NS
                l = b % NL
                rows = slice(b * 128, (b + 1) * 128)
                lo, hi = nbrs(b)
                rlo, rhi = lo - b + 1, hi - b + 2
                nr_ = rhi - rlo
                X, Y = bank[2 * s], bank[2 * s + 1]
                Xb = X[:, :].bitcast(BF16)
                T.append(lambda: P.op(V, lambda e: e.memset(st3[s][:], 0.0), writes=[R_st3[s]]))
                vin = sub(v3[l][:, rlo, 0:1], 0, [[128, nr_], [64, 2], [1, 64]])
                T.append(lambda: P.op("gpsimd", lambda e: e.tensor_copy(out=vaug[s][:, rlo:rhi, :, 0:64], in_=vin),
                                      reads=[R_v3[l]], writes=[R_vaug[s]]))
                T.append(lambda: P.group("tensor", [(lambda e, k=k: e.transpose(out=Xb[:, k * 128:(k + 1) * 128], in_=gin[l][:, k * 128:(k + 1) * 128],
                                                                               identity=ident_b[:])) for k in range(4)],
                                         reads=[R_gin[l], R_ident], writes=[R_X[s]]))
                T.append(lambda: P.op("scalar", lambda e: e.activation(out=gT[s][:].rearrange("p a b -> p (a b)"), in_=Xb[:, 0:512], func=AF.Copy),
                                      reads=[R_X[s]], writes=[R_gT[s]]))
                T.append(lambda: P.group("tensor", [(lambda e, k=k: e.matmul(Y[:, :], lhsT=gT[s][:, k, :], rhs=w_glu_bf[:, k, :],
                                                                            start=(k == 0), stop=(k == 3))) for k in range(4)],
                                         reads=[R_gT[s], R_c], writes=[R_Y[s]]))
                T.append(lambda: P.op(V, lambda e: e.tensor_tensor(out=f1[s][:], in0=Y[:, :], in1=bglu[:], op=ALU.add),
                                      reads=[R_Y[s], R_c], writes=[R_f1[s]]))
                T.append(lambda: P.op("scalar", lambda e: e.activation(out=f1[s][:], in_=f1[s][:], func=AF.Exp, scale=-1.0), reads=[R_f1[s]], writes=[R_f1[s]]))
                T.append(lambda: P.op(V, lambda e: e.tensor_scalar(out=f1[s][:], in0=f1[s][:], scalar1=1.0, scalar2=None, op0=ALU.add), reads=[R_f1[s]], writes=[R_f1[s]]))
                T.append(lambda: P.op(V, lambda e: e.reciprocal(out=f1[s][:], in_=f1[s][:]), reads=[R_f1[s]], writes=[R_f1[s]]))
                T.append(lambda: P.op(V, lambda e: e.tensor_tensor(out=f1[s][:], in0=f1[s][:], in1=gin[l][:], op=ALU.mult),
                                      reads=[R_f1[s], R_gin[l]], writes=[R_f1[s]]))
                T.append(lambda: P.op("scalar", lambda e: e.activation(out=junk3[s][:, 0:512], in_=f1[s][:], func=AF.Square, accum_out=st3[s][:, 0:1]),
                                      reads=[R_f1[s]], writes=[R_junk3[s], R_st3[s]]))
                rstd_ops(T, s, 0, 512)
                T.append(lambda: P.op("scalar", lambda e: e.activation(out=szs[s][:], in_=pin[l][:], func=AF.Exp, scale=-1.0), reads=[R_pin[l]], writes=[R_szs[s]]))
                T.append(lambda: P.op(V, lambda e: e.tensor_scalar(out=szs[s][:, 0:512], in0=szs[s][:, 0:512], scalar1=1.0, scalar2=None, op0=ALU.add),
                                      reads=[R_szs[s]], writes=[R_szs[s]]))
                T.append(lambda: P.op(V, lambda e: e.reciprocal(out=szs[s][:, 0:512], in_=szs[s][:, 0:512]), reads=[R_szs[s]], writes=[R_szs[s]]))
                T.append(lambda: P.op("scalar", lambda e: e.activation(out=szs[s][:, 512:1024], in_=szs[s][:, 512:1024], func=AF.Ln, bias=1.0), reads=[R_szs[s]], writes=[R_szs[s]]))
                T.append(lambda: P.op("scalar", lambda e: e.activation(out=szs[s][:, 512:1024], in_=szs[s][:, 512:1024], func=AF.Exp, scale=-1.0), reads=[R_szs[s]], writes=[R_szs[s]]))
                T.append(lambda: P.op("gpsimd", lambda e: e.tensor_tensor(out=szs[s][:], in0=szs[s][:], in1=pin[l][:], op=ALU.mult),
                                      reads=[R_szs[s], R_pin[l]], writes=[R_szs[s]]))
                T.append(lambda: P.op(V, lambda e: e.scalar_tensor_tensor(out=f1[s][:], in0=f1[s][:], scalar=st3[s][:, 1:2], in1=nws[:],
                                                                          op0=ALU.mult, op1=ALU.mult), reads=[R_f1[s], R_st3[s], R_c], writes=[R_f1[s]]))
                T.append(lambda: P.op("gpsimd", lambda e: e.tensor_tensor(out=mixed[s][:, 0:512], in0=f1[s][:], in1=szs[s][:, 0:512], op=ALU.mult),
                                      reads=[R_f1[s], R_szs[s]], writes=[R_mixed[s]]))
                for kvh in range(2):
                    E, RE = Et[s][kvh], R_Et[s][kvh]
                    for ri in range(nr_):
                        rel = rlo + ri
                        fns = [
                            (lambda e, rel=rel, kvh=kvh: e.matmul(
                                X[:, 0:512], lhsT=kT[l][:, kvh, rel * 128:(rel + 1) * 128],
                                rhs=qT[l][:, 2 * kvh:2 * kvh + 2, :, :].rearrange("p a b c -> p (a b c)"), start=True, stop=False)),
                            (lambda e, rel=rel, kvh=kvh: e.matmul(
                                X[:, 0:512], lhsT=ident_b[:], rhs=expbT[:, kvh, rel, :, :].rearrange("p a b -> p (a b)"),
                                start=False, stop=True)),
                        ]
                        T.append(lambda fns=fns: P.group("tensor", fns, reads=[R_kT[l], R_qT[l], R_c, R_ident], writes=[R_X[s]]))
                        T.append(lambda E=E, RE=RE, rel=rel: P.op("scalar", lambda e: e.activation(out=E[:, rel, :], in_=X[:, 0:512], func=AF.Exp, scale=0.125),
                                                                  reads=[R_X[s]], writes=[RE]))
                    fns = []
                    for hg in range(4):
                        for ri in range(nr_):
                            rel = rlo + ri
                            fns.append(lambda e, hg=hg, E=E, rel=rel, kvh=kvh, ri=ri: e.matmul(
                                Y[:, hg * 65:(hg + 1) * 65], lhsT=E[:, rel, hg * 128:(hg + 1) * 128], rhs=vaug[s][:, rel, kvh, 0:65],
                                start=(ri == 0), stop=(ri == nr_ - 1)))
                    T.append(lambda fns=fns, RE=RE: P.group("tensor", fns, reads=[RE, R_vaug[s]], writes=[R_Y[s]]))
                    dcol = 8 + kvh * 4
                    pOd = sub(Y[:, 64:65], 0, [[65, 4]])
                    T.append(lambda pOd=pOd, dcol=dcol, kvh=kvh: P.op(V, lambda e: e.tensor_tensor(
                        out=st3[s][:, dcol:dcol + 4], in0=pOd, in1=esink[:, kvh * 4:(kvh + 1) * 4], op=ALU.add),
                        reads=[R_Y[s], R_c], writes=[R_st3[s]]))
                    T.append(lambda dcol=dcol: P.op(V, lambda e: e.reciprocal(out=st3[s][:, dcol:dcol + 4], in_=st3[s][:, dcol:dcol + 4]),
                                                    reads=[R_st3[s]], writes=[R_st3[s]]))
                    pOv = sub(Y[:, 0:1], 0, [[65, 4], [1, 64]])
                    rcv = sub(st3[s][:, dcol:dcol + 1], 0, [[1, 4], [0, 64]])
                    T.append(lambda pOv=pOv, rcv=rcv, kvh=kvh: P.op(V, lambda e: e.tensor_tensor(
                        out=ya[s][:, kvh * 256:(kvh + 1) * 256].rearrange("p (a b) -> p a b", b=64), in0=pOv, in1=rcv, op=ALU.mult),
                        reads=[R_Y[s], R_st3[s]], writes=[R_ya[s]]))
                T.append(lambda: P.op("scalar", lambda e: e.activation(out=junk3[s][:, 0:512], in_=ya[s][:], func=AF.Square, accum_out=st3[s][:, 2:3]),
                                      reads=[R_ya[s]], writes=[R_junk3[s], R_st3[s]]))
                rstd_ops(T, s, 2, 512)
                T.append(lambda: P.op(V, lambda e: e.scalar_tensor_tensor(out=ya[s][:], in0=ya[s][:], scalar=st3[s][:, 3:4], in1=nwa[:],
                                                                          op0=ALU.mult, op1=ALU.mult), reads=[R_ya[s], R_st3[s], R_c], writes=[R_ya[s]]))
                T.append(lambda: P.op("gpsimd", lambda e: e.tensor_tensor(out=mixed[s][:, 512:1024], in0=ya[s][:], in1=szs[s][:, 512:1024], op=ALU.mult),
                                      reads=[R_ya[s], R_szs[s]], writes=[R_mixed[s]]))
                T.append(lambda: P.group("tensor", [(lambda e, k=k: e.transpose(out=Xb[:, k * 128:(k + 1) * 128], in_=mixed[s][:, k * 128:(k + 1) * 128],
                                                                               identity=ident_b[:])) for k in range(8)],
                                         reads=[R_mixed[s], R_ident], writes=[R_X[s]]))
                T.append(lambda: P.op(V, lambda e: e.tensor_copy(out=mT[s][:].rearrange("p a b -> p (a b)"), in_=Xb[:, :]),
                                      reads=[R_X[s]], writes=[R_mT[s]]))
                for nb2, (BK, RB) in enumerate(((Y, R_Y[s]), (X, R_X[s]))):
                    T.append(lambda nb2=nb2, BK=BK, RB=RB: P.group("tensor", [(lambda e, k=k: e.matmul(
                        BK[:, :], lhsT=mT[s][:, k, :], rhs=w_out_bf[:, k, nb2 * 512:(nb2 + 1) * 512], start=(k == 0), stop=(k == 7))) for k in range(8)],
                        reads=[R_mT[s], R_c], writes=[RB]))
                    T.append(lambda nb2=nb2, BK=BK, RB=RB: P.op(V, lambda e: e.tensor_tensor(
                        out=rr[s][:, nb2 * 512:(nb2 + 1) * 512], in0=BK[:, :], in1=xr[l][:, nb2 * 512:(nb2 + 1) * 512], op=ALU.add),
                        reads=[RB, R_xr[l]], writes=[R_rr[s]]))
                T.append(lambda: P.op("scalar", lambda e: e.activation(out=junk3[s][:], in_=rr[s][:], func=AF.Square, accum_out=st3[s][:, 4:5]),
                                      reads=[R_rr[s]], writes=[R_junk3[s], R_st3[s]]))
                rstd_ops(T, s, 4, D)
                T.append(lambda: P.op(V, lambda e: e.scalar_tensor_tensor(out=rr[s][:], in0=rr[s][:], scalar=st3[s][:, 5:6], in1=fnw[:],
                                                                          op0=ALU.mult, op1=ALU.mult), reads=[R_rr[s], R_st3[s], R_c], writes=[R_rr[s]]))
                T.append(lambda: st_toks.append(P.dma("gpsimd", lambda e: e.dma_start(out=y[rows, :], in_=rr[s][:]), f"yo{s}", reads=[R_rr[s]], writes=[R_rr[s]])))
                return T

            sched = []
            nblk = NB if upto >= 3 else 0
            for b in range(nblk):
                L = []
                if b == 0:
                    L += loads(0) + (loads(1) if nblk > 1 else [])
                if b + 2 < nblk:
                    L += loads(b + 2)
                L += compute(b)
                n = len(L)
                for i, th in enumerate(L):
                    sched.append((b / NS + i / n, b, i, th))
            sched.sort(key=lambda t: (t[0], t[1], t[2]))
            for _, _, _, th in sched:
                th()
            P.wait_all("gpsimd", st_toks[-NS:])
        P.barrier()
        P.replay()
    return nc


PARAM_KEYS = ["norm_w", "w_in", "lam_re", "lam_im", "log_step", "b_re", "b_im", "c_re", "c_im", "d_skip", "w_glu", "b_glu",
              "ssm_norm_w", "sink", "attn_norm_w", "w_out", "rel_bias", "final_norm_w"]


def param_map(inputs):
    f = lambda a: np.ascontiguousarray(np.asarray(a, dtype=np.float32))
    m = {}
    m["norm_w"] = f(inputs["norm_w"]).reshape(1, D)
    m["w_in"] = f(inputs["w_in"]).reshape(D, DP)
    m["lam_re"] = f(inputs["lam_re"]).reshape(2, 32, 64)
    m["lam_im"] = f(inputs["lam_im"]).reshape(2, 32, 64)
    m["log_step"] = f(inputs["log_step"]).reshape(1, 64)
    m["b_re"] = f(inputs["b_re"]).reshape(2, 32, 64, 16)
    m["b_im"] = f(inputs["b_im"]).reshape(2, 32, 64, 16)
    m["c_re"] = f(inputs["c_re"]).reshape(2, 32, 16, 64)
    m["c_im"] = f(inputs["c_im"]).reshape(2, 32, 16, 64)
    m["d_skip"] = f(inputs["d_skip"]).reshape(1, 512)
    m["w_glu"] = f(inputs["w_glu"]).reshape(512, 512)
    m["b_glu"] = f(inputs["b_glu"]).reshape(1, 512)
    m["ssm_norm_w"] = f(inputs["ssm_norm_w"]).reshape(1, 512)
    m["sink"] = f(inputs["sink"]).reshape(1, 8)
    m["attn_norm_w"] = f(inputs["attn_norm_w"]).reshape(1, 512)
    m["w_out"] = f(inputs["w_out"]).reshape(D, D)
    m["rel_bias"] = f(inputs["rel_bias"]).reshape(32, 8)
    m["final_norm_w"] = f(inputs["final_norm_w"]).reshape(1, D)
    m.update(host_consts())
    return m


def kernel(**inputs):
    xp = np.asarray(inputs["x_prompt"], dtype=np.float32)
    xs = np.asarray(inputs["x_sample"], dtype=np.float32)
    n = 8
    seqs = [2048] * 4 + [4096] * 2
    nc = build_nc(seqs)
    pm = param_map(inputs)
    in_maps = []
    for i in range(n):
        xc = np.concatenate([xp[4 * i:4 * i + 4].reshape(-1, D), xs[2 * i:2 * i + 2].reshape(-1, D)], axis=0)
        m = dict(pm)
        m["x"] = np.ascontiguousarray(xc)
        in_maps.append(m)
    res = run_bass_kernel_spmd(nc, in_maps, core_ids=list(range(n)))
    yp = np.empty_like(xp)
    ys = np.empty_like(xs)
    for i in range(n):
        yc = np.asarray(res.results[i]["y"], dtype=np.float32)
        yp[4 * i:4 * i + 4] = yc[:8192].reshape(4, 2048, D)
        ys[2 * i:2 * i + 2] = yc[8192:].reshape(2, 4096, D)
    return (yp, ys)
```

```python
from contextlib import ExitStack
import numpy as np
import concourse.bass as bass
import concourse.mybir as mybir
from concourse.bass_utils import run_bass_kernel_spmd

F32 = mybir.dt.float32
BF16 = mybir.dt.bfloat16
I32 = mybir.dt.int32
AF = mybir.ActivationFunctionType
ALU = mybir.AluOpType

D = 1024
DP = 2304
PJ = 1664
PI = float(np.pi)
TWO_PI = float(2 * np.pi)


class Tok:
    __slots__ = ("sem", "val", "key")

    def __init__(self, sem, val, key):
        self.sem, self.val, self.key = sem, val, key


class Res:
    __slots__ = ("w", "r")

    def __init__(self):
        self.w = None
        self.r = []


class Prog:
    ENGS = ("sync", "scalar", "vector", "gpsimd", "tensor")

    def __init__(self, nc, stack):
        self.nc = nc
        self.stack = stack
        self.lists = {e: [] for e in self.ENGS}
        self.esem = {e: stack.enter_context(nc.semaphore("es_" + e)) for e in self.ENGS}
        self.ecnt = {e: 0 for e in self.ENGS}
        self.seen = {e: {} for e in self.ENGS}
        self.dsems = {}

    def _deps(self, reads, writes):
        deps = []
        for r in reads:
            if r.w is not None:
                deps.append(r.w)
        for w in writes:
            if w.w is not None:
                deps.append(w.w)
            deps.extend(w.r)
        return deps

    def _waits(self, eng, deps):
        best = {}
        for d in deps:
            if d is None:
                continue
            if d.key not in best or best[d.key].val < d.val:
                best[d.key] = d
        for k, d in best.items():
            if self.seen[eng].get(k, 0) < d.val:
                self.lists[eng].append(("w", d.sem, d.val))
                self.seen[eng][k] = d.val

    def _mark(self, tok, reads, writes):
        for r in reads:
            r.r.append(tok)
        for w in writes:
            w.w = tok
            w.r = []

    def op(self, eng, fn, reads=(), writes=(), extra=()):
        self._waits(eng, self._deps(reads, writes) + list(extra))
        self.ecnt[eng] += 1
        t = Tok(self.esem[eng], self.ecnt[eng], "e_" + eng)
        self.lists[eng].append(("i", fn, self.esem[eng], 1))
        self._mark(t, reads, writes)
        return t

    def group(self, eng, fns, reads=(), writes=()):
        self._waits(eng, self._deps(reads, writes))
        for fn in fns[:-1]:
            self.lists[eng].append(("i", fn, None, 0))
        self.ecnt[eng] += 1
        t = Tok(self.esem[eng], self.ecnt[eng], "e_" + eng)
        self.lists[eng].append(("i", fns[-1], self.esem[eng], 1))
        self._mark(t, reads, writes)
        return t

    def dma(self, eng, fn, dname, reads=(), writes=()):
        if dname not in self.dsems:
            self.dsems[dname] = [self.stack.enter_context(self.nc.semaphore("ds_" + dname)), 0]
        self._waits(eng, self._deps(reads, writes))
        ds = self.dsems[dname]
        ds[1] += 16
        self.lists[eng].append(("i", fn, ds[0], 16))
        t = Tok(ds[0], ds[1], "d_" + dname)
        self._mark(t, reads, writes)
        return t

    def barrier(self):
        toks = [Tok(self.esem[e], self.ecnt[e], "e_" + e) for e in self.ENGS if self.ecnt[e] > 0]
        toks += [Tok(ds[0], ds[1], "d_" + nm) for nm, ds in self.dsems.items()]
        for e in self.ENGS:
            self._waits(e, toks)

    def wait_all(self, eng, toks):
        self._waits(eng, toks)

    def replay(self):
        nc = self.nc
        lists = self.lists

        def run(e, items):
            pend = []
            for it in items:
                if it[0] == "w":
                    pend.append(it)
                    continue
                for w in pend[:-1]:
                    e.wait_ge(w[1], w[2])
                ins = it[1](e)
                if pend:
                    ins._wait_ge(pend[-1][1], pend[-1][2])
                pend = []
                if it[2] is not None:
                    ins.then_inc(it[2], it[3])
            for w in pend:
                e.wait_ge(w[1], w[2])

        with nc.Block() as block:
            @block.sync
            def _(e):
                run(e, lists["sync"])

            @block.scalar
            def _(e):
                run(e, lists["scalar"])

            @block.vector
            def _(e):
                run(e, lists["vector"])

            @block.gpsimd
            def _(e):
                run(e, lists["gpsimd"])

            @block.tensor
            def _(e):
                run(e, lists["tensor"])


def mkap(t, offset, dims):
    tens = t.tensor if hasattr(t, "tensor") else t
    return bass.AP(tens, offset, [list(d) for d in dims])


def sub(ap, extra_off, dims):
    return mkap(ap, ap.offset + extra_off, [list(ap.ap[0])] + [list(d) for d in dims])


def t5_bucket_np(rel):
    half = 16
    max_exact = 8
    ret = np.where(rel > 0, half, 0)
    n = np.abs(rel)
    nf = np.maximum(n, 1).astype(np.float32)
    large = max_exact + (np.log(nf / np.float32(max_exact)) / np.float32(np.log(128 / max_exact))
                         * np.float32(half - max_exact)).astype(np.int32)
    large = np.minimum(large, half - 1)
    return ret + np.where(n < max_exact, n, large)


def host_consts():
    c = {}
    import ml_dtypes
    c["c_ident"] = np.eye(128, dtype=np.float32)
    c["c_exch"] = np.eye(128, dtype=np.float32)[::-1].copy()
    jj = np.arange(128) // 16
    c["c_maskf"] = (jj[None, :] >= jj[:, None]).astype(np.float32)
    c["c_maskb"] = (jj[None, :] <= jj[:, None]).astype(np.float32)
    ng = np.zeros((128, 2, 40), np.float32)
    ng[:, 0, :] = np.arange(-7, 33)[None, :]
    ng[:, 1, :] = (32 - np.arange(40))[None, :]
    c["c_ng"] = ng.reshape(128, 80)
    c["c_cg"] = np.tile(np.arange(128, dtype=np.float32)[None, :], (128, 1))
    i = np.arange(511)
    delta = 255 - i
    bk = t5_bucket_np(delta)
    oh = np.zeros((32, 512), np.float32)
    oh[bk, i] = 1.0
    c["c_oh"] = oh
    valid = np.zeros((8, 512), np.float32)
    valid[:, :511] = (np.abs(delta) <= 128).astype(np.float32)[None, :]
    c["c_valid"] = valid
    return c


def build_nc(seqs, upto=9):
    NTOK = sum(seqs)
    NB = NTOK // 128
    NC = NTOK // 32
    NCT = NC // 128
    assert NC % 128 == 0
    blk_seq = []
    for si, L in enumerate(seqs):
        blk_seq += [si] * (L // 128)
    segs = []
    cs = 0
    i = 0
    while i < len(seqs):
        j = i
        while j < len(seqs) and seqs[j] == seqs[i]:
            j += 1
        segs.append((cs, j - i, seqs[i] // 32))
        cs += (j - i) * (seqs[i] // 32)
        i = j
    NSEQ = len(seqs)
    PADC = NC + NSEQ

    nc = bass.Bass("TRN2", target_bir_lowering=False)
    dt_in = lambda n, s, d=F32: nc.dram_tensor(n, list(s), d, kind="ExternalInput").ap()
    x = dt_in("x", [NTOK, D])
    norm_w = dt_in("norm_w", [1, D])
    w_in = dt_in("w_in", [D, DP])
    lam_re = dt_in("lam_re", [2, 32, 64])
    lam_im = dt_in("lam_im", [2, 32, 64])
    log_step = dt_in("log_step", [1, 64])
    b_re = dt_in("b_re", [2, 32, 64, 16])
    b_im = dt_in("b_im", [2, 32, 64, 16])
    c_re = dt_in("c_re", [2, 32, 16, 64])
    c_im = dt_in("c_im", [2, 32, 16, 64])
    d_skip = dt_in("d_skip", [1, 512])
    w_glu = dt_in("w_glu", [512, 512])
    b_glu = dt_in("b_glu", [1, 512])
    ssm_norm_w = dt_in("ssm_norm_w", [1, 512])
    sink = dt_in("sink", [1, 8])
    attn_norm_w = dt_in("attn_norm_w", [1, 512])
    w_out = dt_in("w_out", [D, D])
    rel_bias = dt_in("rel_bias", [32, 8])
    final_norm_w = dt_in("final_norm_w", [1, D])
    c_ident = dt_in("c_ident", [128, 128])
    c_exch = dt_in("c_exch", [128, 128])
    c_maskf = dt_in("c_maskf", [128, 128])
    c_maskb = dt_in("c_maskb", [128, 128])
    c_ng = dt_in("c_ng", [128, 80])
    c_cg = dt_in("c_cg", [128, 128])
    c_oh = dt_in("c_oh", [32, 512])
    c_valid = dt_in("c_valid", [8, 512])
    y = nc.dram_tensor("y", [NTOK, D], F32, kind="ExternalOutput").ap()
    proj = nc.dram_tensor("proj", [NTOK, PJ], BF16, kind="Internal").ap()
    qkT = nc.dram_tensor("qkT", [768, NTOK], BF16, kind="Internal").ap()
    gd = nc.dram_tensor("gd", [NTOK, 512], BF16, kind="Internal").ap()
    wd = nc.dram_tensor("wd", [8, 512], F32, kind="Internal").ap()

    with ExitStack() as st:
        P = Prog(nc, st)
        sb = lambda n, s, d=F32, stk=st: stk.enter_context(nc.sbuf_tensor(n, list(s), d))
        ps = lambda n, s, d=F32, stk=st: stk.enter_context(nc.psum_tensor(n, list(s), d))

        def bcast_rows(src, n):
            return mkap(src, 0, [[0, 128], [1, n]])

        ident_f = sb("ident_f", [128, 128])
        ident_b = sb("ident_b", [128, 128], BF16)
        R_ident = Res()
        P.dma("sync", lambda e: e.dma_start(out=ident_f[:], in_=c_ident[:, :]), "c0", writes=[R_ident])
        P.op("vector", lambda e: e.tensor_copy(out=ident_b[:], in_=ident_f[:]), reads=[R_ident], writes=[R_ident])
        R_proj = [Res() for _ in range(NB)]
        R_qkT = [Res() for _ in range(NB)]
        R_gdp = [[[Res() for _ in range(4)] for _ in range(8)] for _ in range(4)]

        sT = ExitStack()
        sbT = lambda n, s, d=F32: sb(n, s, d, sT)
        V = "vector"
        LR = sbT("LR", [128, 64]); LI = sbT("LI", [128, 64]); DT = sbT("DT", [128, 64])
        AR = sbT("AR", [128, 64]); TH = sbT("TH", [128, 64])
        DSK = sbT("DSK", [128, 512])
        NG = sbT("NG", [128, 2, 40]); CG = sbT("CG", [128, 128])
        MASKF = sbT("MASKF", [128, 128]); MASKB = sbT("MASKB", [128, 128])
        PRE = sbT("PRE", [128, 64, 40]); PIM = sbT("PIM", [128, 64, 40])
        X1 = sbT("X1", [128, 64, 16]); X2 = sbT("X2", [128, 64, 16])
        Y1 = sbT("Y1", [128, 64, 16]); Y2 = sbT("Y2", [128, 64, 16])
        M32 = sbT("M32", [128, 64]); TH32 = sbT("TH32", [128, 64])
        R_T = Res()
        with ExitStack() as s1:
            sb1 = lambda n, s, d=F32: sb(n, s, d, s1)
            ps1 = lambda n, s, d=F32: ps(n, s, d, s1)
            w_in_bf = sb1("w_in_bf", [128, 8, DP], BF16)
            wstage = sb1("wstage", [128, DP])
            nwT = sb1("nwT", [128, 8])
            R_w, R_ws, R_nw = Res(), Res(), Res()
            P.dma("sync", lambda e: e.dma_start(out=nwT[:], in_=mkap(norm_w, 0, [[1, 128], [128, 8]]),
                                                allow_slow_non_contiguous=True), "c1", writes=[R_nw])
            for kc in range(8):
                P.dma("sync", lambda e, kc=kc: e.dma_start(out=wstage[:], in_=w_in[kc * 128:(kc + 1) * 128, :]),
                      "ws", writes=[R_ws])
                P.op("vector", lambda e, kc=kc: e.tensor_scalar(out=w_in_bf[:, kc, :], in0=wstage[:],
                                                                scalar1=nwT[:, kc:kc + 1], scalar2=None, op0=ALU.mult),
                     reads=[R_ws, R_nw], writes=[R_w])
            w_kd = sb1("w_kd", [128, 8, 2, 128], BF16)
            for kvh in range(2):
                for cpy in range(2):
                    P.op("vector", lambda e, kvh=kvh, cpy=cpy: e.tensor_copy(
                        out=w_kd[:, :, kvh, cpy * 64:(cpy + 1) * 64], in_=w_in_bf[:, :, 1536 + kvh * 64:1536 + (kvh + 1) * 64]),
                         reads=[R_w], writes=[R_w])
            NX = 3
            xt = [sb1(f"xt{i}", [128, D]) for i in range(NX)]
            xb = [sb1(f"xb{i}", [128, D], BF16) for i in range(2)]
            xT = [sb1(f"xT{i}", [128, 8, 128], BF16) for i in range(2)]
            pj = [sb1(f"pj{i}", [128, PJ], BF16) for i in range(2)]
            qk = [sb1(f"qk{i}", [128, 6, 128], BF16) for i in range(2)]
            junk = [sb1(f"junk{i}", [128, D], BF16) for i in range(2)]
            ss = [sb1(f"ss{i}", [128, 2]) for i in range(2)]
            R_xt = [Res() for _ in range(NX)]
            R_xb, R_xT, R_pj, R_qk, R_junk, R_ss = ([Res(), Res()] for _ in range(6))
            ptr = [ps1(f"ptr1{i}", [128, 1024], BF16) for i in range(2)]
            R_ptr = [Res(), Res()]
            pp = [ps1(f"pp{i}", [128, 512]) for i in range(4)]
            R_pp = [Res() for _ in range(4)]
            pq = ps1("pq", [128, 1024])
            R_pq = [Res(), Res()]
            colblocks = [(0, 512, 0), (512, 512, 512), (1664, 512, 1024), (2176, 128, 1536)]

            def s1_load(b):
                lx = b % NX
                return [lambda: P.dma("sync", lambda e: e.dma_start(out=xt[lx][:], in_=x[b * 128:(b + 1) * 128, :]),
                                      f"x{lx}", writes=[R_xt[lx]])]

            def s1_block(b):
                T = []
                s = b % 2
                lx = b % NX
                T.append(lambda: P.op("vector", lambda e: e.memset(ss[s][:], 0.0), writes=[R_ss[s]]))
                T.append(lambda: P.op("scalar", lambda e: e.activation(out=junk[s][:], in_=xt[lx][:], func=AF.Square, accum_out=ss[s][:, 0:1]),
                                      reads=[R_xt[lx]], writes=[R_junk[s], R_ss[s]]))
                T.append(lambda: P.op("scalar", lambda e: e.activation(out=ss[s][:, 1:2], in_=ss[s][:, 0:1], func=AF.Ln, scale=1.0 / D, bias=1e-6),
                                      reads=[R_ss[s]], writes=[R_ss[s]]))
                T.append(lambda: P.op("scalar", lambda e: e.activation(out=ss[s][:, 1:2], in_=ss[s][:, 1:2], func=AF.Exp, scale=-0.5),
                                      reads=[R_ss[s]], writes=[R_ss[s]]))
                T.append(lambda: P.op("vector", lambda e: e.tensor_scalar(out=xb[s][:], in0=xt[lx][:], scalar1=ss[s][:, 1:2], scalar2=None, op0=ALU.mult),
                                      reads=[R_xt[lx], R_ss[s]], writes=[R_xb[s]]))
                T.append(lambda: P.group("tensor", [(lambda e, k=k: e.transpose(out=ptr[s][:, k * 128:(k + 1) * 128], in_=xb[s][:, k * 128:(k + 1) * 128],
                                                                               identity=ident_b[:])) for k in range(8)],
                                         reads=[R_xb[s], R_ident], writes=[R_ptr[s]]))
                T.append(lambda: P.op("scalar", lambda e: e.activation(out=xT[s][:].rearrange("p a b -> p (a b)"), in_=ptr[s][:, :], func=AF.Copy),
                                      reads=[R_ptr[s]], writes=[R_xT[s]]))
                for nb, (c0, cw, pc) in enumerate(colblocks):
                    T.append(lambda nb=nb, c0=c0, cw=cw: P.group("tensor", [(lambda e, k=k: e.matmul(
                        pp[nb][:, 0:cw], lhsT=xT[s][:, k, :], rhs=w_in_bf[:, k, c0:c0 + cw], start=(k == 0), stop=(k == 7))) for k in range(8)],
                        reads=[R_xT[s], R_w], writes=[R_pp[nb]]))
                    if nb % 2 == 0:
                        T.append(lambda nb=nb, pc=pc, cw=cw: P.op("vector", lambda e: e.tensor_copy(out=pj[s][:, pc:pc + cw], in_=pp[nb][:, 0:cw]),
                                                                  reads=[R_pp[nb]], writes=[R_pj[s]]))
                    else:
                        T.append(lambda nb=nb, pc=pc, cw=cw: P.op("scalar", lambda e: e.activation(out=pj[s][:, pc:pc + cw], in_=pp[nb][:, 0:cw], func=AF.Copy),
                                                                  reads=[R_pp[nb]], writes=[R_pj[s]]))
                for half in range(2):
                    fns = []
                    for j in (range(0, 4) if half == 0 else range(4, 6)):
                        for k in range(8):
                            w_ap = w_in_bf[:, k, 1024 + j * 128:1024 + (j + 1) * 128] if j < 4 else w_kd[:, k, j - 4, :]
                            fns.append(lambda e, k=k, j=j, w_ap=w_ap: e.matmul(
                                pq[:, j * 128:(j + 1) * 128], lhsT=w_ap, rhs=xT[s][:, k, :], start=(k == 0), stop=(k == 7)))
                    T.append(lambda fns=fns, half=half: P.group("tensor", fns, reads=[R_xT[s], R_w], writes=[R_pq[half]]))
                    if half == 0:
                        T.append(lambda: P.op("vector", lambda e: e.tensor_copy(out=qk[s][:, 0:4, :].rearrange("p a b -> p (a b)"), in_=pq[:, 0:512]),
                                              reads=[R_pq[0]], writes=[R_qk[s]]))
                    else:
                        T.append(lambda: P.op("scalar", lambda e: e.activation(out=qk[s][:, 4:6, :].rearrange("p a b -> p (a b)"), in_=pq[:, 512:768],
                                                                               func=AF.Copy), reads=[R_pq[1]], writes=[R_qk[s]]))
                T.append(lambda: P.dma("gpsimd", lambda e: e.dma_start(out=proj[b * 128:(b + 1) * 128, :], in_=pj[s][:]),
                                       f"pj{s}", reads=[R_pj[s]], writes=[R_proj[b]]))
                dq = mkap(qkT, b * 128, [[NTOK, 128], [128 * NTOK, 6], [1, 128]])
                T.append(lambda: P.dma("gpsimd", lambda e: e.dma_start(out=dq, in_=qk[s][:]), f"qk{s}", reads=[R_qk[s]], writes=[R_qkT[b]]))
                return T

            def record1(body):
                saved = (P.op, P.group, P.dma)
                L = []
                P.op = lambda *a, **k: L.append((a[0], lambda: saved[0](*a, **k)))
                P.group = lambda *a, **k: L.append((a[0], lambda: saved[1](*a, **k)))
                P.dma = lambda *a, **k: L.append(("dma", lambda: saved[2](*a, **k)))
                return L, saved

            SETUP2, _saved = record1(None)
            s2a = ExitStack()
            sb2a = lambda n, s, d=F32: sb(n, s, d, s2a)
            BR = sb2a("BR", [128, 64, 16]); BI = sb2a("BI", [128, 64, 16])
            CR = sb2a("CR", [128, 64, 16]); CI = sb2a("CI", [128, 64, 16])
            TMPA = sb2a("TMPA", [128, 2560]); TMPB = sb2a("TMPB", [128, 2560]); TMPI = sb2a("TMPI", [128, 2560], I32)
            MAG = sb2a("MAG", [128, 2560])
            sm = [sb2a(f"sm{i}", [128, 64]) for i in range(8)]
            CRraw = sb2a("CRraw", [128, 8, 2, 64]); LRraw = sb2a("LRraw", [64, 2, 64])

            def ld(dst, src_ap, name, slow=False):
                P.dma("sync", lambda e: e.dma_start(out=dst, in_=src_ap, allow_slow_non_contiguous=slow), "t", writes=[R_T])

            for h in range(2):
                hs = slice(h * 64, (h + 1) * 64)
                ld(BR[hs, :, :], mkap(b_re, 0, [[16, 64], [1024, 64], [1, 16]]), f"t2{h}")
                ld(BI[hs, :, :], mkap(b_im, 0, [[16, 64], [1024, 64], [1, 16]]), f"t3{h}")
            ld(DT[:], bcast_rows(log_step, 64), "t6")
            ld(DSK[:], bcast_rows(d_skip, 512), "t7")
            ld(NG[:].rearrange("p a b -> p (a b)"), c_ng[:, :], "t8")
            ld(CG[:], c_cg[:, :], "t9")
            ld(MASKF[:], c_maskf[:, :], "t10")
            ld(MASKB[:], c_maskb[:, :], "t11")

            ptw0 = pq
            R_raw = Res()
            for src_t, dst_t, nm in ((c_re, CR, "a"), (c_im, CI, "b")):
                for cp in range(2):
                    P.dma("sync", lambda e, src_t=src_t, cp=cp: e.dma_start(
                        out=CRraw[:, :, cp, :], in_=mkap(src_t, 0, [[64, 128], [8192, 8], [1, 64]])), "tr", writes=[R_raw])
                P.group("tensor", [(lambda e, a=a: e.transpose(out=ptw0[:, a * 128:(a + 1) * 128],
                                                               in_=CRraw[:, a, :, :].rearrange("p a b -> p (a b)"),
                                                               identity=ident_f[:])) for a in range(8)],
                        reads=[R_raw, R_ident], writes=[R_pq[0], R_pq[1]])
                P.op("vector", lambda e, dst_t=dst_t: e.tensor_copy(out=dst_t[:].rearrange("p a b -> p (a b)"), in_=ptw0[:, :]),
                     reads=[R_pq[0], R_pq[1]], writes=[R_T])
            for src_t, dst_t, nm in ((lam_re, LR, "c"), (lam_im, LI, "d")):
                for cp in range(2):
                    P.dma("sync", lambda e, src_t=src_t, cp=cp: e.dma_start(
                        out=LRraw[:, cp, :], in_=mkap(src_t, 0, [[64, 64], [1, 64]])), "tr", writes=[R_raw])
                P.op("tensor", lambda e: e.transpose(out=ptw0[:, 0:64], in_=LRraw[:, :, :].rearrange("p a b -> p (a b)"),
                                                     identity=ident_f[0:64, 0:64]), reads=[R_raw, R_ident], writes=[R_pq[0], R_pq[1]])
                P.op("vector", lambda e, dst_t=dst_t: e.tensor_copy(out=dst_t[:], in_=ptw0[:, 0:64]), reads=[R_pq[0], R_pq[1]], writes=[R_T])

            VX = []

            def vop(fn, eng=V):
                return P.op(eng, fn, reads=[R_T], writes=[R_T] + VX)

            TMPS = [TMPB, TMPI]

            def range_sin(dst, src, n, shift):
                kf = TMPS[0][:, 0:n]
                ki = TMPS[1][:, 0:n]
                vop(lambda e: e.tensor_scalar(out=kf, in0=src, scalar1=float(1.0 / TWO_PI), scalar2=float(shift / TWO_PI), op0=ALU.mult, op1=ALU.add))
                vop(lambda e: e.tensor_copy(out=ki, in_=kf))
                vop(lambda e: e.tensor_copy(out=kf, in_=ki))
                vop(lambda e: e.scalar_tensor_tensor(out=dst, in0=kf, scalar=-TWO_PI, in1=src, op0=ALU.mult, op1=ALU.add))
                vop(lambda e: e.tensor_scalar(out=dst, in0=dst, scalar1=float(-3.14159 - shift), scalar2=float(3.14159 - shift), op0=ALU.max, op1=ALU.min))
                vop(lambda e: e.activation(out=dst, in_=dst, func=AF.Sin, bias=float(shift)), "scalar")

            vop(lambda e: e.activation(out=DT[:], in_=DT[:], func=AF.Exp), "scalar")
            vop(lambda e: e.tensor_tensor(out=AR[:], in0=LR[:], in1=DT[:], op=ALU.mult))
            vop(lambda e: e.tensor_tensor(out=TH[:], in0=LI[:], in1=DT[:], op=ALU.mult))
            vop(lambda e: e.tensor_scalar(out=TH32[:], in0=TH[:], scalar1=32.0, scalar2=None, op0=ALU.mult))
            vop(lambda e: e.activation(out=M32[:], in_=AR[:], func=AF.Exp, scale=32.0), "scalar")
            ngb = mkap(NG, NG[:].offset, [list(NG[:].ap[0]), [40, 2], [0, 32], [1, 40]])
            bc40 = lambda T: mkap(T, T[:].offset, [list(T[:].ap[0]), [32, 2], [1, 32], [0, 40]])
            v4 = lambda T: mkap(T, T[:].offset, [list(T[:].ap[0]), [1280, 2], [40, 32], [1, 40]])
            vop(lambda e: e.tensor_tensor(out=v4(MAG), in0=bc40(AR), in1=ngb, op=ALU.mult))
            vop(lambda e: e.activation(out=MAG[:], in_=MAG[:], func=AF.Exp), "scalar")
            vop(lambda e: e.tensor_tensor(out=v4(TMPA), in0=bc40(TH), in1=ngb, op=ALU.mult))
            PREf = PRE[:].rearrange("p a b -> p (a b)")
            PIMf = PIM[:].rearrange("p a b -> p (a b)")
            range_sin(PIMf, TMPA[:], 2560, 0.0)
            range_sin(PREf, TMPA[:], 2560, PI / 2)
            vop(lambda e: e.tensor_tensor(out=PREf, in0=PREf, in1=MAG[:], op=ALU.mult))
            vop(lambda e: e.tensor_tensor(out=PIMf, in0=PIMf, in1=MAG[:], op=ALU.mult))
            lbre, lbim, den, nr, cre, cim, t0_, t1_ = sm
            for dd, idx in ((0, 8), (1, 31)):
                gs = slice(dd * 32, (dd + 1) * 32)
                vop(lambda e, gs=gs, idx=idx: e.tensor_copy(out=lbre[:, gs], in_=PRE[:, gs, idx]))
                vop(lambda e, gs=gs, idx=idx: e.tensor_copy(out=lbim[:, gs], in_=PIM[:, gs, idx]))
            vop(lambda e: e.tensor_tensor(out=den[:], in0=LR[:], in1=LR[:], op=ALU.mult))
            vop(lambda e: e.tensor_tensor(out=t0_[:], in0=LI[:], in1=LI[:], op=ALU.mult))
            vop(lambda e: e.tensor_tensor(out=den[:], in0=den[:], in1=t0_[:], op=ALU.add))
            vop(lambda e: e.reciprocal(out=den[:], in_=den[:]))
            vop(lambda e: e.tensor_scalar(out=nr[:], in0=lbre[:], scalar1=-1.0, scalar2=None, op0=ALU.add))
            vop(lambda e: e.tensor_tensor(out=cre[:], in0=nr[:], in1=LR[:], op=ALU.mult))
            vop(lambda e: e.tensor_tensor(out=t0_[:], in0=lbim[:], in1=LI[:], op=ALU.mult))
            vop(lambda e: e.tensor_tensor(out=cre[:], in0=cre[:], in1=t0_[:], op=ALU.add))
            vop(lambda e: e.tensor_tensor(out=cre[:], in0=cre[:], in1=den[:], op=ALU.mult))
            vop(lambda e: e.tensor_tensor(out=cim[:], in0=lbim[:], in1=LR[:], op=ALU.mult))
            vop(lambda e: e.tensor_tensor(out=t0_[:], in0=nr[:], in1=LI[:], op=ALU.mult))
            vop(lambda e: e.tensor_tensor(out=cim[:], in0=cim[:], in1=t0_[:], op=ALU.subtract))
            vop(lambda e: e.tensor_tensor(out=cim[:], in0=cim[:], in1=den[:], op=ALU.mult))
            bc16 = lambda T: mkap(T, T[:].offset, [list(T[:].ap[0]), [1, 64], [0, 16]])
            BBre = mkap(TMPA, TMPA[:].offset, [list(TMPA[:].ap[0]), [16, 64], [1, 16]])
            BBim = mkap(TMPA, TMPA[:].offset + 1024, [list(TMPA[:].ap[0]), [16, 64], [1, 16]])
            Tm = mkap(TMPB, TMPB[:].offset, [list(TMPB[:].ap[0]), [16, 64], [1, 16]])
            vop(lambda e: e.tensor_tensor(out=BBre, in0=bc16(cre), in1=BR[:], op=ALU.mult))
            vop(lambda e: e.tensor_tensor(out=Tm, in0=bc16(cim), in1=BI[:], op=ALU.mult))
            vop(lambda e: e.tensor_tensor(out=BBre, in0=BBre, in1=Tm, op=ALU.subtract))
            vop(lambda e: e.tensor_tensor(out=BBim, in0=bc16(cre), in1=BI[:], op=ALU.mult))
            vop(lambda e: e.tensor_tensor(out=Tm, in0=bc16(cim), in1=BR[:], op=ALU.mult))
            vop(lambda e: e.tensor_tensor(out=BBim, in0=BBim, in1=Tm, op=ALU.add))

            def hv(T3, h):
                return T3[h * 64:(h + 1) * 64, :, :]

            def hva(apfull, h):
                base = TMPA[h * 64:(h + 1) * 64, :]
                return mkap(TMPA, base.offset + (apfull.offset - TMPA[:].offset), [list(base.ap[0]), [16, 64], [1, 16]])

            vop(lambda e: e.tensor_copy(out=hv(X1, 0), in_=hva(BBre, 0)))
            vop(lambda e: e.tensor_copy(out=hv(X1, 1), in_=hva(BBim, 1)))
            vop(lambda e: e.tensor_scalar(out=hv(X2, 0), in0=hva(BBim, 0), scalar1=-1.0, scalar2=None, op0=ALU.mult))
            vop(lambda e: e.tensor_copy(out=hv(X2, 1), in_=hva(BBre, 1)))
            vop(lambda e: e.tensor_copy(out=hv(Y1, 0), in_=hv(CR, 0)))
            vop(lambda e: e.tensor_scalar(out=hv(Y1, 1), in0=hv(CI, 1), scalar1=-1.0, scalar2=None, op0=ALU.mult))
            vop(lambda e: e.tensor_scalar(out=hv(Y2, 0), in0=hv(CI, 0), scalar1=-1.0, scalar2=None, op0=ALU.mult))
            vop(lambda e: e.tensor_scalar(out=hv(Y2, 1), in0=hv(CR, 1), scalar1=-1.0, scalar2=None, op0=ALU.mult))

            P.op, P.group, P.dma = _saved
            sched1 = []
            for b in range(NB):
                L = (s1_load(0) if b == 0 else []) + (s1_load(b + 1) if b + 1 < NB else []) + s1_block(b)
                n = len(L)
                for i, th in enumerate(L):
                    sched1.append((b * 0.5 + i / n, b, i, th))
            merged = []
            i = 0
            while i < len(SETUP2):
                if SETUP2[i][0] == "tensor" and i + 1 < len(SETUP2):
                    f1_, f2_ = SETUP2[i][1], SETUP2[i + 1][1]
                    merged.append(lambda f1_=f1_, f2_=f2_: (f1_(), f2_()))
                    i += 2
                else:
                    merged.append(SETUP2[i][1])
                    i += 1
            for i, th in enumerate(merged):
                sched1.append((0.26 + 12.0 * i / max(1, len(merged)), -1, i, th))
            sched1.sort(key=lambda t: (t[0], t[1], t[2]))
            for _, _, _, th in sched1:
                th()
            s2a.close()

        P.barrier()
        with ExitStack() as s2:
          if upto >= 2:
            sb2 = lambda n, s, d=F32: sb(n, s, d, s2)
            ps2 = lambda n, s, d=F32: ps(n, s, d, s2)
            V = "vector"
            T4all = sb2("T4all", [128, 2560]); T4pall = sb2("T4pall", [128, 2560])
            TMPG = T4all[:, 0:2048]
            TMPS[0] = T4pall[:, 0:1024]
            TMPS[1] = T4pall[:, 1024:2048].bitcast(I32)
            XG = sb2("XG", [128, NCT, 32, 128], BF16)
            XP = sb2("XP", [128, NCT, 4, 32, 16], BF16)
            R_XP = Res()
            UG = sb2("UG", [128, 4, 4, NC], BF16)
            YG = sb2("YG", [128, 4, 4, NC], BF16)
            COSR = sb2("COSR", [128, 8, 2, 128]); SINR = sb2("SINR", [128, 8, 2, 128])
            T4 = [T4all[:, i * 1280:(i + 1) * 1280].rearrange("p (a b) -> p a b", b=640) for i in range(2)]
            T4p = [T4pall[:, i * 1280:(i + 1) * 1280].rearrange("p (a b) -> p a b", b=640) for i in range(2)]
            R_T4p = Res()
            WP = []
            for wi in range(2):
                WP.append({"t": (sb2(f"EN{wi}", [128, 2, 512], BF16), sb2(f"ES{wi}", [128, 2, 512], BF16),
                                 sb2(f"HA{wi}", [128, 2, 640], BF16), sb2(f"HB{wi}", [128, 2, 640], BF16),
                                 sb2(f"WA{wi}", [128, 16, 128], BF16), sb2(f"TW{wi}", [128, 7, 128], BF16),
                                 sb2(f"m0{wi}", [128, 128]), sb2(f"m1{wi}", [128, 128])),
                           "r": (Res(), Res(), Res(), Res(), Res())})
            tA = sb2("tA", [128, NC]); tB = sb2("tB", [128, NC]); Rr = tA
            SBf = [sb2(f"SBf{d}", [128, PADC]) for d in range(2)]
            RCs = [[sb2(f"RC{p}{d}", [128, PADC], BF16) for d in range(2)] for p in range(2)]
            RSns = [[sb2(f"RSn{p}{d}", [128, PADC], BF16) for d in range(2)] for p in range(2)]
            R_rcs = [Res(), Res()]
            RC, RSn = RCs[0], RSns[0]
            dummy = sb2("dummy_t", [128, 1])
            ptr2 = ps2("ptr2", [128, 1024], BF16)
            ptw = ps2("ptw", [128, 512])
            R_ptw = Res()
            py = [ps2(f"py{i}", [128, 512]) for i in range(2)]
            R_py = [Res(), Res()]
            pz = [ps2(f"pz{i}", [128, 512]) for i in range(4)]
            R_UG, R_YG, R_rot = Res(), Res(), Res()
            R_XGp = [[Res() for _ in range(4)] for _ in range(NCT)]
            R_T4 = Res()
            R_ptr2 = Res()
            R_pz = [Res() for _ in range(4)]
            R_st = Res()
            R_rc = R_rcs[0]
            for d in range(2):
                P.op(V, lambda e, d=d: e.memset(SBf[d][:], 0.0), writes=[R_st])
                for p_ in range(2):
                    P.op(V, lambda e, d=d, p_=p_: e.memset(RCs[p_][d][:], 0.0), writes=[R_rcs[p_]])
                    P.op(V, lambda e, d=d, p_=p_: e.memset(RSns[p_][d][:], 0.0), writes=[R_rcs[p_]])

            def seg_views():
                out = []
                sidx = 0
                for (c0, ns, ncs) in segs:
                    out.append((c0, ns, ncs, c0 + sidx))
                    sidx += ns
                return out
            SEGS = seg_views()

            def grp(part, g, g4, g8, W):
                EN, ES, HA, HB, WA, TW, m0, m1 = W["t"]
                R_EN, R_HA, R_WA, R_TW, R_m = W["r"]
                if part == "w":
                    def pw_e(T):
                        return mkap(T, T[:].offset + g * 40 + 38, [list(T[:].ap[0]), [1274, 2], [-1, 32], [0, 16]])

                    def xv_e(T):
                        return mkap(T, T[:].offset + g * 16, [list(T[:].ap[0]), [512, 2], [0, 32], [1, 16]])

                    def pw_h(T):
                        return mkap(T, T[:].offset + g * 40, [list(T[:].ap[0]), [1280, 2], [1, 40], [0, 16]])

                    def xv_h(T):
                        return mkap(T, T[:].offset + g * 16, [list(T[:].ap[0]), [512, 2], [0, 40], [1, 16]])

                    def t4e(i):
                        return mkap(T4[i], T4[i][:].offset, [list(T4[i][:].ap[0]), [640, 2], [16, 32], [1, 16]])

                    def t4h(i):
                        return mkap(T4[i], T4[i][:].offset, [list(T4[i][:].ap[0]), [640, 2], [16, 40], [1, 16]])

                    def e3(T, n):
                        return mkap(T, T[:].offset, [list(T[:].ap[0]), [T[:].ap[1][0], 2], [16, n], [1, 16]])

                    rw = dict(reads=[R_T, R_EN], writes=[R_T4])
                    P.op(V, lambda e, a=pw_e(PRE), b=xv_e(X1): e.tensor_tensor(out=t4e(0), in0=a, in1=b, op=ALU.mult), **rw)
                    P.op(V, lambda e, a=pw_e(PIM), b=xv_e(X2): e.tensor_tensor(out=t4e(1), in0=a, in1=b, op=ALU.mult), **rw)
                    P.op(V, lambda e: e.tensor_tensor(out=e3(EN, 32), in0=t4e(0), in1=t4e(1), op=ALU.add),
                         reads=[R_T4], writes=[R_EN])
                    P.op(V, lambda e, a=pw_e(PRE), b=xv_e(X2): e.tensor_tensor(out=t4e(0), in0=a, in1=b, op=ALU.mult), **rw)
                    P.op(V, lambda e, a=pw_e(PIM), b=xv_e(X1): e.tensor_tensor(out=t4e(1), in0=a, in1=b, op=ALU.mult), **rw)
                    P.op(V, lambda e: e.tensor_tensor(out=ES[:, 0, :], in0=T4[1][:, 0, 0:512], in1=T4[0][:, 0, 0:512],
                                                      op=ALU.subtract), reads=[R_T4], writes=[R_EN])
                    P.op(V, lambda e: e.tensor_tensor(out=ES[:, 1, :], in0=T4[0][:, 1, 0:512], in1=T4[1][:, 1, 0:512],
                                                      op=ALU.subtract), reads=[R_T4], writes=[R_EN])
                    MARK("H")
                    def t4hp(i):
                        return mkap(T4p[i], T4p[i][:].offset, [list(T4p[i][:].ap[0]), [640, 2], [16, 40], [1, 16]])
                    G = "gpsimd"
                    rwp = dict(reads=[R_T, R_HA], writes=[R_T4p])
                    P.op(G, lambda e, a=pw_h(PRE), b=xv_h(Y1): e.tensor_tensor(out=t4hp(0), in0=a, in1=b, op=ALU.mult), **rwp)
                    P.op(G, lambda e, a=pw_h(PIM), b=xv_h(Y2): e.tensor_tensor(out=t4hp(1), in0=a, in1=b, op=ALU.mult), **rwp)
                    P.op(G, lambda e: e.tensor_tensor(out=HA[:], in0=T4p[0][:], in1=T4p[1][:], op=ALU.add),
                         reads=[R_T4p], writes=[R_HA])
                    P.op(G, lambda e, a=pw_h(PRE), b=xv_h(Y2): e.tensor_tensor(out=t4hp(0), in0=a, in1=b, op=ALU.mult), **rwp)
                    P.op(G, lambda e, a=pw_h(PIM), b=xv_h(Y1): e.tensor_tensor(out=t4hp(1), in0=a, in1=b, op=ALU.mult), **rwp)
                    P.op(G, lambda e: e.tensor_tensor(out=HB[:, 0, :], in0=T4p[0][:, 0, :], in1=T4p[1][:, 0, :],
                                                      op=ALU.subtract), reads=[R_T4p], writes=[R_HA])
                    P.op(G, lambda e: e.tensor_tensor(out=HB[:, 1, :], in0=T4p[1][:, 1, :], in1=T4p[0][:, 1, :],
                                                      op=ALU.subtract), reads=[R_T4p], writes=[R_HA])
                    MARK("W")
                    for ns, Tsrc in ((0, EN), (1, ES)):
                        fns = []
                        for d in range(2):
                            for jb in range(4):
                                k = d * 4 + jb
                                fns.append(lambda e, Tsrc=Tsrc, d=d, jb=jb, k=k: e.transpose(
                                    out=ptr2[:, k * 128:(k + 1) * 128], in_=Tsrc[:, d, jb * 128:(jb + 1) * 128],
                                    identity=ident_b[:]))
                        P.group("tensor", fns, reads=[R_EN, R_ident], writes=[R_ptr2])
                        P.op("scalar", lambda e, ns=ns: e.activation(
                            out=WA[:, ns * 8:(ns + 1) * 8, :].rearrange("p a b -> p (a b)"), in_=ptr2[:, :], func=AF.Copy),
                             reads=[R_ptr2], writes=[R_WA])
                    MARK("T")
                    fns = []
                    for dl in range(4):
                        fns.append(lambda e, dl=dl: e.matmul(ptw[:, dl * 128:(dl + 1) * 128], lhsT=EN[:, 0, 384:512],
                                                             rhs=HA[:, 0, dl * 128:(dl + 1) * 128], start=True, stop=True))
                    P.group("tensor", fns, reads=[R_EN, R_HA], writes=[R_ptw])
                    P.op(V, lambda e: e.tensor_copy(out=TW[:, 4:7, :].rearrange("p a b -> p (a b)"), in_=ptw[:, 128:512]),
                         reads=[R_ptw], writes=[R_TW])
                    P.op(V, lambda e: e.tensor_tensor(out=m0[:], in0=ptw[:, 0:128], in1=MASKF[:], op=ALU.mult),
                         reads=[R_ptw, R_T], writes=[R_m])
                    fns = []
                    for dl in range(4):
                        fns.append(lambda e, dl=dl: e.matmul(ptw[:, dl * 128:(dl + 1) * 128], lhsT=EN[:, 1, 0:128],
                                                             rhs=HA[:, 1, (32 - 8 * dl) * 16:(40 - 8 * dl) * 16],
                                                             start=True, stop=True))
                    P.group("tensor", fns, reads=[R_EN, R_HA], writes=[R_ptw])
                    for dl in range(1, 4):
                        P.op(V, lambda e, dl=dl: e.tensor_copy(out=TW[:, 3 - dl, :], in_=ptw[:, dl * 128:(dl + 1) * 128]),
                             reads=[R_ptw], writes=[R_TW])
                    P.op(V, lambda e: e.tensor_tensor(out=m1[:], in0=ptw[:, 0:128], in1=MASKB[:], op=ALU.mult),
                         reads=[R_ptw, R_T], writes=[R_m])
                    P.op(V, lambda e: e.tensor_tensor(out=m0[:], in0=m0[:], in1=m1[:], op=ALU.add), reads=[R_m], writes=[R_m])
                    dskv = mkap(DSK, DSK[:].offset + g * 16, [list(DSK[:].ap[0]), [0, 8], [1, 16]])
                    P.op(V, lambda e, dskv=dskv: e.tensor_tensor(out=m1[:].rearrange("p (a b) -> p a b", b=16),
                                                                 in0=ident_f[:].rearrange("p (a b) -> p a b", b=16),
                                                                 in1=dskv, op=ALU.mult), reads=[R_T, R_ident], writes=[R_m])
                    P.op(V, lambda e: e.tensor_tensor(out=TW[:, 3, :], in0=m0[:], in1=m1[:], op=ALU.add),
                         reads=[R_m], writes=[R_TW])
                else:
                    for d in range(2):
                        for ns in range(2):
                            P.group("tensor", [(lambda e, d=d, ns=ns, jb=jb, g4=g4: e.matmul(
                                pz[d * 2 + ns][:, 0:NC], lhsT=WA[:, ns * 8 + d * 4 + jb, :], rhs=UG[:, g4, jb, :],
                                start=(jb == 0), stop=(jb == 3))) for jb in range(4)],
                                    reads=[R_WA, R_UG], writes=[R_pz[d * 2 + ns]])
                    for d in range(2):
                        for (c0, nsq, ncs, pc0) in SEGS:
                            def zv(T, c0=c0, nsq=nsq, ncs=ncs):
                                return sub(T[:, c0:c0 + nsq * ncs], 0, [[ncs, nsq], [1, ncs]])

                            def tbv(T, g8=g8, d=d, nsq=nsq, ncs=ncs):
                                return sub(T[:, g8, d, 0:ncs], 0, [[0, nsq], [1, ncs]])
                            P.op(V, lambda e, o=zv(tA), i0=zv(pz[d * 2]), i1=tbv(COSR): e.tensor_tensor(out=o, in0=i0, in1=i1, op=ALU.mult),
                                 reads=[R_pz[d * 2], R_rot], writes=[R_st])
                            P.op(V, lambda e, o=zv(tB), i0=zv(pz[d * 2 + 1]), i1=tbv(SINR): e.tensor_tensor(out=o, in0=i0, in1=i1, op=ALU.mult),
                                 reads=[R_pz[d * 2 + 1], R_rot], writes=[R_st])
                            P.op(V, lambda e, o=zv(Rr), i0=zv(tA), i1=zv(tB): e.tensor_tensor(out=o, in0=i0, in1=i1, op=ALU.add),
                                 reads=[R_st], writes=[R_st])
                            dec = sub(M32[:, d * 32 + g:d * 32 + g + 1], 0, [[0, ncs]])
                            for sq in range(nsq):
                                pcol = pc0 + sq * (ncs + 1)
                                ccol = c0 + sq * ncs
                                if d == 0:
                                    o_ap = SBf[0][:, pcol + 1:pcol + 1 + ncs]
                                    i_ap = Rr[:, ccol:ccol + ncs]
                                else:
                                    o_ap = sub(SBf[1][:, pcol + ncs - 1:pcol + ncs], 0, [[-1, ncs]])
                                    i_ap = sub(Rr[:, ccol + ncs - 1:ccol + ncs], 0, [[-1, ncs]])
                                P.op(V, lambda e, o_ap=o_ap, i_ap=i_ap, dec=dec: e.tensor_tensor_scan(
                                    out=o_ap, data0=dec, data1=i_ap, initial=0.0, op0=ALU.mult, op1=ALU.add),
                                     reads=[R_st, R_T], writes=[R_st])
                            off = 1 if d == 0 else 0

                            def sv(T, pc0=pc0, off=off, ncs=ncs, nsq=nsq):
                                return sub(T[:, pc0 + off:pc0 + off + 1], 0, [[ncs + 1, nsq], [1, ncs]])
                            P.op(V, lambda e, o=sv(RC[d]), i0=sv(SBf[d]), i1=tbv(COSR): e.tensor_tensor(out=o, in0=i0, in1=i1, op=ALU.mult),
                                 reads=[R_st, R_rot], writes=[R_rc])
                            P.op(V, lambda e, o=sv(RSn[d]), i0=sv(SBf[d]), i1=tbv(SINR): e.tensor_tensor(out=o, in0=i0, in1=i1, op=ALU.mult),
                                 reads=[R_st, R_rot], writes=[R_rc])
                    MARK("R")
                    for tb_ in range(4):
                        fns = []
                        for (c0, nsq, ncs, pc0) in SEGS:
                            for sq in range(nsq):
                                cc = c0 + sq * ncs
                                osl = py[tb_ % 2][:, cc:cc + ncs]
                                for jb in range(4):
                                    fns.append(lambda e, osl=osl, w=TW[:, tb_ - jb + 3, :], r=UG[:, g4, jb, cc:cc + ncs], jb=jb: e.matmul(
                                        osl, lhsT=w, rhs=r, start=(jb == 0), stop=False))
                                for d in range(2):
                                    off = 0 if d == 0 else 1
                                    if d == 0:
                                        wa = HA[:, 0, (8 * tb_ + 8) * 16:(8 * tb_ + 16) * 16]
                                        wb = HB[:, 0, (8 * tb_ + 8) * 16:(8 * tb_ + 16) * 16]
                                    else:
                                        wa = HA[:, 1, (8 * tb_) * 16:(8 * tb_ + 8) * 16]
                                        wb = HB[:, 1, (8 * tb_) * 16:(8 * tb_ + 8) * 16]
                                    pcol = pc0 + sq * (ncs + 1) + off
                                    last = (d == 1)
                                    fns.append(lambda e, osl=osl, wa=wa, r=RC[d][:, pcol:pcol + ncs]: e.matmul(
                                        osl, lhsT=wa, rhs=r, start=False, stop=False))
                                    fns.append(lambda e, osl=osl, wb=wb, r=RSn[d][:, pcol:pcol + ncs], last=last: e.matmul(
                                        osl, lhsT=wb, rhs=r, start=False, stop=last))
                        P.group("tensor", fns, reads=[R_TW, R_UG, R_HA, R_rc], writes=[R_py[tb_ % 2]])
                        P.op("scalar", lambda e, tb_=tb_, g4=g4: e.activation(out=YG[:, g4, tb_, :], in_=py[tb_ % 2][:, 0:NC], func=AF.Copy),
                             reads=[R_py[tb_ % 2]], writes=[R_YG])

            REC = [None]

            def MARK(name):
                if REC[0] is not None:
                    REC[0].append(("mark", name))

            def record(body):
                saved_op, saved_group = P.op, P.group
                L = []
                REC[0] = L
                P.op = lambda *a, **k: L.append(("th", lambda: saved_op(*a, **k)))
                P.group = lambda *a, **k: L.append(("th", lambda: saved_group(*a, **k)))
                try:
                    body()
                finally:
                    P.op, P.group = saved_op, saved_group
                    REC[0] = None
                secs = {"": []}
                cur = ""
                for kind, v in L:
                    if kind == "mark":
                        cur = v
                        secs.setdefault(cur, [])
                    else:
                        secs[cur].append(v)
                return secs

            def emit_interleaved(*lists):
                items = []
                for li, L in enumerate(lists):
                    for i, th in enumerate(L):
                        items.append(((i + 0.5) / len(L), li, i, th))
                items.sort(key=lambda t: (t[0], t[1], t[2]))
                for _, _, _, th in items:
                    th()

            def rec_w(g):
                sec = record(lambda: grp("w", g, 0, 0, WP[g % 2]))
                return sec[""] + sec["W"], sec["H"] + sec["T"]

            def rec_s(g, g4, g8):
                nonlocal RC, RSn, R_rc
                RC, RSn, R_rc = RCs[g % 2], RSns[g % 2], R_rcs[g % 2]
                sec = record(lambda: grp("s", g, g4, g8, WP[g % 2]))
                return sec[""], sec["R"]

            wE0, wH0 = rec_w(0)
            for th in wE0 + wH0:
                th()

            for gb in range(4):
                for ct in range(NCT):
                    for tq in range(4):
                        src = mkap(proj, (128 * ct * 32 + tq * 8) * PJ + gb * 128, [[32 * PJ, 128], [PJ, 8], [1, 128]])
                        tk = P.dma("sync", lambda e, src=src, ct=ct, tq=tq: e.dma_start(out=XG[:, ct, tq * 8:(tq + 1) * 8, :], in_=src),
                                   f"xg{ct}", reads=R_proj, writes=[R_XGp[ct][tq]])
                    for tq in range(4):
                        R_XGp[ct][tq].w = tk
                thv = mkap(TH32, TH32[:].offset + gb * 8, [list(TH32[:].ap[0]), [1, 8], [32, 2], [0, 128]])
                cgv = mkap(CG, CG[:].offset, [list(CG[:].ap[0]), [0, 8], [0, 2], [1, 128]])
                ta4 = mkap(TMPG, TMPG[:].offset, [list(TMPG[:].ap[0]), [256, 8], [128, 2], [1, 128]])
                VX[:] = [R_T4, R_T4p]
                P.op(V, lambda e, thv=thv, cgv=cgv, ta4=ta4: e.tensor_tensor(out=ta4, in0=thv, in1=cgv, op=ALU.mult),
                     reads=[R_T], writes=[R_T, R_T4, R_T4p])
                P.op(V, lambda e: e.memset(dummy[:], 0.0), reads=[R_T], writes=[R_rot, R_T])
                for hh in range(2):
                    range_sin(SINR[:].rearrange("p a b c -> p (a b c)")[:, hh * 1024:(hh + 1) * 1024], TMPG[:, hh * 1024:(hh + 1) * 1024], 1024, 0.0)
                    range_sin(COSR[:].rearrange("p a b c -> p (a b c)")[:, hh * 1024:(hh + 1) * 1024], TMPG[:, hh * 1024:(hh + 1) * 1024], 1024, PI / 2)
                P.op(V, lambda e: e.memset(dummy[:], 0.0), reads=[R_T], writes=[R_rot, R_T])
                VX[:] = []
                for hb in range(2):
                    def relayout(hb_):
                        for ct in range(NCT):
                            xin = sub(XG[:, ct, 0, hb_ * 64:hb_ * 64 + 1], 0, [[16, 4], [128, 32], [1, 16]])
                            P.op(V, lambda e, xin=xin, ct=ct: e.tensor_copy(out=XP[:, ct, :, :, :], in_=xin),
                                 reads=R_XGp[ct], writes=[R_XP])
                    if hb == 0:
                        relayout(0)
                    for ct in range(NCT):
                        for g2 in range(0, 4, 2):
                            fns = []
                            for gi in range(2):
                                for jb in range(4):
                                    src_ap = XP[:, ct, g2 + gi, 8 * jb:8 * jb + 8, :].rearrange("p a b -> p (a b)")
                                    k = gi * 4 + jb
                                    fns.append(lambda e, src_ap=src_ap, k=k: e.transpose(
                                        out=ptr2[:, k * 128:(k + 1) * 128], in_=src_ap, identity=ident_b[:]))
                            P.group("tensor", fns, reads=[R_XP, R_ident], writes=[R_ptr2])
                            dst = sub(UG[:, g2, 0, ct * 128:(ct + 1) * 128], 0, [[NC, 8], [1, 128]])
                            if (ct * 2 + g2 // 2) % 2 == 0:
                                P.op(V, lambda e, dst=dst: e.tensor_copy(out=dst, in_=ptr2[:, :].rearrange("p (a b) -> p a b", b=128)),
                                     reads=[R_ptr2], writes=[R_UG])
                            else:
                                P.op("scalar", lambda e, dst=dst: e.activation(
                                    out=dst, in_=ptr2[:, :].rearrange("p (a b) -> p a b", b=128), func=AF.Copy),
                                     reads=[R_ptr2], writes=[R_UG])
                    if hb == 0:
                        relayout(1)
                    for g4 in range(4):
                        g8 = hb * 4 + g4
                        g = gb * 8 + g8
                        sA, sB = rec_s(g, g4, g8)
                        wE, wH = rec_w(g + 1) if g + 1 < 32 else ([], [])
                        if g4 == 0:
                            emit_interleaved(sA, wE)
                        else:
                            emit_interleaved(prev_sB, sA, wE)
                        for th in wH:
                            th()
                        prev_sB = sB
                    for th in prev_sB:
                        th()
                    for ct in range(NCT):
                        for g2 in range(0, 4, 2):
                            fns = []
                            for gi in range(2):
                                for tb_ in range(4):
                                    k = gi * 4 + tb_
                                    fns.append(lambda e, gi=gi, tb_=tb_, k=k, ct=ct, g2=g2: e.transpose(
                                        out=ptr2[:, k * 128:(k + 1) * 128], in_=YG[:, g2 + gi, tb_, ct * 128:(ct + 1) * 128],
                                        identity=ident_b[:]))
                            P.group("tensor", fns, reads=[R_YG, R_ident], writes=[R_ptr2])
                            for gi in range(2):
                                g8 = hb * 4 + g2 + gi
                                dst = sub(XG[:, ct, 0, 16 * g8:16 * g8 + 1], 0, [[1024, 4], [128, 8], [1, 16]])
                                srcp = sub(ptr2[:, gi * 512:gi * 512 + 1], 0, [[128, 4], [16, 8], [1, 16]])
                                P.op("scalar", lambda e, dst=dst, srcp=srcp: e.activation(out=dst, in_=srcp, func=AF.Gelu_apprx_tanh),
                                     reads=[R_ptr2], writes=R_XGp[ct])
                for ct in range(NCT):
                    for tq in range(4):
                        dstd = mkap(gd, (128 * ct * 32 + tq * 8) * 512 + gb * 128, [[32 * 512, 128], [512, 8], [1, 128]])
                        tk = P.dma("gpsimd", lambda e, dstd=dstd, ct=ct, tq=tq: e.dma_start(out=dstd, in_=XG[:, ct, tq * 8:(tq + 1) * 8, :]),
                                   f"gs{ct}", reads=[R_XGp[ct][tq]], writes=[R_gdp[gb][ct][tq]])
                    for tq in range(4):
                        R_gdp[gb][ct][tq].w = tk
                        for tq2 in range(4):
                            R_XGp[ct][tq2].r.append(tk)

        sT.close()
        P.barrier()
        with ExitStack() as s3:
          if upto >= 2.5:
            sb3 = lambda n, s, d=F32: sb(n, s, d, s3)
            ps3 = lambda n, s, d=F32: ps(n, s, d, s3)
            V = "vector"
            R_c = Res()
            w_out_bf = sb3("w_out_bf", [128, 8, D], BF16)
            w_glu_bf = sb3("w_glu_bf", [128, 4, 512], BF16)
            bglu = sb3("bglu", [128, 512]); nws = sb3("nws", [128, 512]); nwa = sb3("nwa", [128, 512])
            fnw = sb3("fnw", [128, D]); esink = sb3("esink", [128, 8])
            expbT = sb3("expbT", [128, 2, 3, 4, 128], BF16)
            sinkrow = sb3("sinkrow", [1, 8, 65], BF16); onesrow = sb3("onesrow", [1, 128], BF16)
            s3a = ExitStack()
            sb3a = lambda n, s, d=F32: sb(n, s, d, s3a)
            wst = sb3a("wst", [128, D])
            oh_f = sb3a("oh_f", [32, 512]); oh_b = sb3a("oh_b", [32, 512], BF16)
            rb_f = sb3a("rb_f", [32, 8]); rb_b = sb3a("rb_b", [32, 8], BF16)
            vld = sb3a("vld", [8, 512]); wv = sb3a("wv", [8, 512])
            exch_f = sb3a("exch_f", [128, 128]); exch_b = sb3a("exch_b", [128, 128], BF16)
            t2f = sb3a("t2f", [128, 8, 128]); t2b = sb3a("t2b", [128, 8, 128], BF16)
            R_wst = Res()
            for kc in range(8):
                P.dma("sync", lambda e, kc=kc: e.dma_start(out=wst[:], in_=w_out[kc * 128:(kc + 1) * 128, :]), "wst", writes=[R_wst])
                P.op(V, lambda e, kc=kc: e.tensor_copy(out=w_out_bf[:, kc, :], in_=wst[:]), reads=[R_wst], writes=[R_c])
            for kc in range(4):
                P.dma("sync", lambda e, kc=kc: e.dma_start(out=wst[:, 0:512], in_=w_glu[kc * 128:(kc + 1) * 128, :]), "wst", writes=[R_wst])
                P.op(V, lambda e, kc=kc: e.tensor_copy(out=w_glu_bf[:, kc, :], in_=wst[:, 0:512]), reads=[R_wst], writes=[R_c])
            for dst, src, n, nm in ((bglu, b_glu, 512, "k0"), (nws, ssm_norm_w, 512, "k1"), (nwa, attn_norm_w, 512, "k2"),
                                    (fnw, final_norm_w, D, "k3"), (esink, sink, 8, "k4")):
                P.dma("sync", lambda e, dst=dst, src=src, n=n: e.dma_start(out=dst[:], in_=bcast_rows(src, n)), "k", writes=[R_c])
            P.op("scalar", lambda e: e.activation(out=esink[:], in_=esink[:], func=AF.Exp), reads=[R_c], writes=[R_c])
            bank = [ps3(f"bank{i}", [128, 512]) for i in range(8)]
            pA = bank[0]
            R_pA = Res()
            R_wd = Res()
            P.dma("sync", lambda e: e.dma_start(out=oh_f[:], in_=c_oh[:, :]), "k", writes=[R_c])
            P.dma("sync", lambda e: e.dma_start(out=rb_f[:], in_=rel_bias[:, :]), "k", writes=[R_c])
            P.dma("sync", lambda e: e.dma_start(out=vld[:], in_=c_valid[:, :]), "k", writes=[R_c])
            P.dma("sync", lambda e: e.dma_start(out=exch_f[:], in_=c_exch[:, :]), "k", writes=[R_c])
            P.op(V, lambda e: e.tensor_copy(out=oh_b[:], in_=oh_f[:]), reads=[R_c], writes=[R_c])
            P.op(V, lambda e: e.tensor_copy(out=rb_b[:], in_=rb_f[:]), reads=[R_c], writes=[R_c])
            P.op(V, lambda e: e.tensor_copy(out=exch_b[:], in_=exch_f[:]), reads=[R_c], writes=[R_c])
            P.op("tensor", lambda e: e.matmul(pA[0:8, :], lhsT=rb_b[:], rhs=oh_b[:], start=True, stop=True), reads=[R_c], writes=[R_pA])
            P.op("scalar", lambda e: e.activation(out=wv[:], in_=pA[0:8, :], func=AF.Copy, scale=8.0), reads=[R_pA], writes=[R_c])
            P.op(V, lambda e: e.tensor_tensor(out=wv[:], in0=wv[:], in1=vld[:], op=ALU.mult), reads=[R_c], writes=[R_c])
            P.op(V, lambda e: e.tensor_scalar(out=vld[:], in0=vld[:], scalar1=240000.0, scalar2=-240000.0, op0=ALU.mult, op1=ALU.add),
                 reads=[R_c], writes=[R_c])
            P.op(V, lambda e: e.tensor_tensor(out=wv[:], in0=wv[:], in1=vld[:], op=ALU.add), reads=[R_c], writes=[R_c])
            P.op(V, lambda e: e.memset(sinkrow[:], 0.0), reads=[R_c], writes=[R_c])
            P.op(V, lambda e: e.memset(onesrow[:], 1.0), reads=[R_c], writes=[R_c])
            P.op(V, lambda e: e.tensor_copy(out=sub(sinkrow[0:1, 0, 64:65], 0, [[65, 8]]), in_=esink[0:1, :]), reads=[R_c], writes=[R_c])
            P.dma("sync", lambda e: e.dma_start(out=wd[:, :], in_=wv[:]), "k", reads=[R_c], writes=[R_wd, R_c])
            for rel in range(3):
                srcw = mkap(wd, (2 - rel) * 128, [[1, 128], [512, 8], [1, 128]])
                P.dma("sync", lambda e, srcw=srcw: e.dma_start(out=t2f[:], in_=srcw), "k", reads=[R_wd], writes=[R_c])
                P.op(V, lambda e: e.tensor_copy(out=t2b[:], in_=t2f[:]), reads=[R_c], writes=[R_c])
                for hh in range(2):
                    P.op("tensor", lambda e, hh=hh: e.matmul(pA[:, :], lhsT=exch_b[:], rhs=t2b[:, hh * 4:(hh + 1) * 4, :].rearrange("p a b -> p (a b)"),
                                                            start=True, stop=True), reads=[R_c], writes=[R_pA])
                    P.op(V, lambda e, hh=hh, rel=rel: e.tensor_copy(out=expbT[:, hh, rel, :, :],
                                                                   in_=pA[:, :].rearrange("p (a b) -> p a b", b=128)),
                         reads=[R_pA], writes=[R_c])
            s3a.close()
            P.barrier()
            NL = 6
            NS = 4
            gin = [sb3(f"gin{i}", [128, 512], BF16) for i in range(NL)]
            pin = [sb3(f"pin{i}", [128, 1024], BF16) for i in range(NL)]
            v3 = [sb3(f"v3{i}", [128, 3, 128], BF16) for i in range(NL)]
            qT = [sb3(f"qT{i}", [128, 4, 2, 128], BF16) for i in range(NL)]
            kT = [sb3(f"kT{i}", [128, 2, 384], BF16) for i in range(NL)]
            xr = [sb3(f"xr{i}", [128, D]) for i in range(NL)]
            R_gin, R_pin, R_v3, R_qT, R_kT, R_xr = ([Res() for _ in range(NL)] for _ in range(6))
            gT = [sb3(f"gT{i}", [128, 4, 128], BF16) for i in range(NS)]
            vaug = [sb3(f"vaug{i}", [128, 3, 2, 80], BF16) for i in range(NS)]
            Et = [[sb3(f"Et{i}_{j}", [128, 3, 512], BF16) for j in range(2)] for i in range(NS)]
            f1 = [sb3(f"f1{i}", [128, 512]) for i in range(NS)]
            ya = [sb3(f"ya{i}", [128, 512]) for i in range(NS)]
            szs = [sb3(f"szs{i}", [128, 1024]) for i in range(NS)]
            mixed = [sb3(f"mixed{i}", [128, D], BF16) for i in range(NS)]
            mT = [sb3(f"mT{i}", [128, 8, 128], BF16) for i in range(NS)]
            rr = [sb3(f"rr{i}", [128, D]) for i in range(NS)]
            junk3_ = sb3("junk3", [128, D], BF16)
            junk3 = [junk3_] * NS
            st3 = [sb3(f"st3{i}", [128, 32]) for i in range(NS)]
            RS = lambda: [Res() for _ in range(NS)]
            R_gT, R_vaug, R_f1, R_ya, R_szs, R_mixed, R_mT, R_rr, R_junk3, R_st3 = (RS() for _ in range(10))
            R_junk3 = [R_junk3[0]] * NS
            R_Et = [[Res(), Res()] for _ in range(NS)]
            R_X, R_Y = RS(), RS()
            for i in range(NS):
                P.op(V, lambda e, i=i: e.memset(vaug[i][:], 1.0), writes=[R_vaug[i]])
            for i in range(NL):
                P.op(V, lambda e, i=i: e.memset(qT[i][:], 0.0), writes=[R_qT[i]])
            st_toks = []

            def nbrs(b):
                lo = b - 1 if (b > 0 and blk_seq[b - 1] == blk_seq[b]) else b
                hi = b + 1 if (b + 1 < NB and blk_seq[b + 1] == blk_seq[b]) else b
                return lo, hi

            def loads(b):
                l = b % NL
                rows = slice(b * 128, (b + 1) * 128)
                lo, hi = nbrs(b)
                rlo, rhi = lo - b + 1, hi - b + 2
                nr_ = rhi - rlo
                ct = b // 32
                nm = f"ld{l}"

                def go():
                    P.dma("sync", lambda e: e.dma_start(out=gin[l][:], in_=gd[rows, :]), nm,
                          reads=[R_gdp[gbx][ct][tqx] for gbx in range(4) for tqx in range(4)], writes=[R_gin[l]])
                    P.dma("sync", lambda e: e.dma_start(out=pin[l][:, 0:512], in_=proj[rows, 512:1024]), nm,
                          reads=[R_proj[b]], writes=[R_pin[l]])
                    P.dma("sync", lambda e: e.dma_start(out=pin[l][:, 512:1024], in_=proj[rows, 1152:1664]), nm,
                          reads=[R_proj[b]])
                    srcv = mkap(proj, lo * 128 * PJ + 1024, [[PJ, 128], [128 * PJ, nr_], [1, 128]])
                    P.dma("sync", lambda e: e.dma_start(out=v3[l][:, rlo:rhi, :], in_=srcv), nm,
                          reads=[R_proj[j] for j in range(lo, hi + 1)], writes=[R_v3[l]])
                    P.wait_all("sync", [R_qT[l].w] + R_qT[l].r)
                    for hf in range(2):
                        srcq = mkap(qkT, hf * 64 * NTOK + b * 128, [[NTOK, 64], [128 * NTOK, 4], [1, 128]])
                        P.dma("sync", lambda e, srcq=srcq, hf=hf: e.dma_start(
                            out=qT[l][hf * 64:(hf + 1) * 64, :, hf, :], in_=srcq), nm, reads=[R_qkT[b]])
                    srck = mkap(qkT, 512 * NTOK + lo * 128, [[NTOK, 128], [128 * NTOK, 2], [1, nr_ * 128]])
                    P.dma("sync", lambda e: e.dma_start(out=kT[l][:, :, rlo * 128:rhi * 128], in_=srck), nm,
                          reads=[R_qkT[j] for j in range(lo, hi + 1)], writes=[R_kT[l]])
                    t = P.dma("sync", lambda e: e.dma_start(out=xr[l][:], in_=x[rows, :]), nm, writes=[R_xr[l]])
                    for R in (R_gin[l], R_pin[l], R_v3[l], R_qT[l], R_kT[l], R_xr[l]):
                        R.w = t
                        R.r = []
                return [go]

            def rstd_ops(T, s, col, n):
                T.append(lambda: P.op("scalar", lambda e: e.activation(out=st3[s][:, col + 1:col + 2], in_=st3[s][:, col:col + 1],
                                                                       func=AF.Ln, scale=1.0 / n, bias=1e-6),
                                      reads=[R_st3[s]], writes=[R_st3[s]]))
                T.append(lambda: P.op("scalar", lambda e: e.activation(out=st3[s][:, col + 1:col + 2], in_=st3[s][:, col + 1:col + 2],
                                                                       func=AF.Exp, scale=-0.5),
                                      reads=[R_st3[s]], writes=[R_st3[s]]))

            def sigmoid_ops(T, ap, rd, wr):
                T.append(lambda: P.op("scalar", lambda e: e.activation(out=ap, in_=ap, func=AF.Exp, scale=-1.0), reads=rd, writes=wr))
                T.append(lambda: P.op("scalar", lambda e: e.activation(out=ap, in_=ap, func=AF.Ln, bias=1.0), reads=wr, writes=wr))
                T.append(lambda: P.op("scalar", lambda e: e.activation(out=ap, in_=ap, func=AF.Exp, scale=-1.0), reads=wr, writes=wr))

            def compute(b):
                T = []
                s = b % NS
                l = b % NL
                rows = slice(b * 128, (b + 1) * 128)
                lo, hi = nbrs(b)
                rlo, rhi = lo - b + 1, hi - b + 2
                nr_ = rhi - rlo
                X, Y = bank[2 * s], bank[2 * s + 1]
                Xb = X[:, :].bitcast(BF16)
                T.append(lambda: P.op(V, lambda e: e.memset(st3[s][:], 0.0), writes=[R_st3[s]]))
                vin = sub(v3[l][:, rlo, 0:1], 0, [[128, nr_], [64, 2], [1, 64]])
                T.append(lambda: P.op("gpsimd", lambda e: e.tensor_copy(out=vaug[s][:, rlo:rhi, :, 0:64], in_=vin),
                                      reads=[R_v3[l]], writes=[R_vaug[s]]))
                T.append(lambda: P.group("tensor", [(lambda e, k=k: e.transpose(out=Xb[:, k * 128:(k + 1) * 128], in_=gin[l][:, k * 128:(k + 1) * 128],
                                                                               identity=ident_b[:])) for k in range(4)],
                                         reads=[R_gin[l], R_ident], writes=[R_X[s]]))
                T.append(lambda: P.op("scalar", lambda e: e.activation(out=gT[s][:].rearrange("p a b -> p (a b)"), in_=Xb[:, 0:512], func=AF.Copy),
                                      reads=[R_X[s]], writes=[R_gT[s]]))
                T.append(lambda: P.group("tensor", [(lambda e, k=k: e.matmul(Y[:, :], lhsT=gT[s][:, k, :], rhs=w_glu_bf[:, k, :],
                                                                            start=(k == 0), stop=(k == 3))) for k in range(4)],
                                         reads=[R_gT[s], R_c], writes=[R_Y[s]]))
                T.append(lambda: P.op(V, lambda e: e.tensor_tensor(out=f1[s][:], in0=Y[:, :], in1=bglu[:], op=ALU.add),
                                      reads=[R_Y[s], R_c], writes=[R_f1[s]]))
                sigmoid_ops(T, f1[s][:], [R_f1[s]], [R_f1[s]])
                T.append(lambda: P.op(V, lambda e: e.tensor_tensor(out=f1[s][:], in0=f1[s][:], in1=gin[l][:], op=ALU.mult),
                                      reads=[R_f1[s], R_gin[l]], writes=[R_f1[s]]))
                T.append(lambda: P.op("scalar", lambda e: e.activation(out=junk3[s][:, 0:512], in_=f1[s][:], func=AF.Square, accum_out=st3[s][:, 0:1]),
                                      reads=[R_f1[s]], writes=[R_junk3[s], R_st3[s]]))
                rstd_ops(T, s, 0, 512)
                T.append(lambda: P.op("scalar", lambda e: e.activation(out=szs[s][:], in_=pin[l][:], func=AF.Exp, scale=-1.0), reads=[R_pin[l]], writes=[R_szs[s]]))
                T.append(lambda: P.op("scalar", lambda e: e.activation(out=szs[s][:], in_=szs[s][:], func=AF.Ln, bias=1.0), reads=[R_szs[s]], writes=[R_szs[s]]))
                T.append(lambda: P.op("scalar", lambda e: e.activation(out=szs[s][:], in_=szs[s][:], func=AF.Exp, scale=-1.0), reads=[R_szs[s]], writes=[R_szs[s]]))
                T.append(lambda: P.op("gpsimd", lambda e: e.tensor_tensor(out=szs[s][:], in0=szs[s][:], in1=pin[l][:], op=ALU.mult),
                                      reads=[R_szs[s], R_pin[l]], writes=[R_szs[s]]))
                T.append(lambda: P.op(V, lambda e: e.scalar_tensor_tensor(out=f1[s][:], in0=f1[s][:], scalar=st3[s][:, 1:2], in1=nws[:],
                                                                          op0=ALU.mult, op1=ALU.mult), reads=[R_f1[s], R_st3[s], R_c], writes=[R_f1[s]]))
                T.append(lambda: P.op("gpsimd", lambda e: e.tensor_tensor(out=mixed[s][:, 0:512], in0=f1[s][:], in1=szs[s][:, 0:512], op=ALU.mult),
                                      reads=[R_f1[s], R_szs[s]], writes=[R_mixed[s]]))
                for kvh in range(2):
                    E, RE = Et[s][kvh], R_Et[s][kvh]
                    for ri in range(nr_):
                        rel = rlo + ri
                        fns = [
                            (lambda e, rel=rel, kvh=kvh: e.matmul(
                                X[:, 0:512], lhsT=kT[l][:, kvh, rel * 128:(rel + 1) * 128],
                                rhs=qT[l][:, 2 * kvh:2 * kvh + 2, :, :].rearrange("p a b c -> p (a b c)"), start=True, stop=False)),
                            (lambda e, rel=rel, kvh=kvh: e.matmul(
                                X[:, 0:512], lhsT=ident_b[:], rhs=expbT[:, kvh, rel, :, :].rearrange("p a b -> p (a b)"),
                                start=False, stop=True)),
                        ]
                        T.append(lambda fns=fns: P.group("tensor", fns, reads=[R_kT[l], R_qT[l], R_c, R_ident], writes=[R_X[s]]))
                        T.append(lambda E=E, RE=RE, rel=rel: P.op("scalar", lambda e: e.activation(out=E[:, rel, :], in_=X[:, 0:512], func=AF.Exp, scale=0.125),
                                                                  reads=[R_X[s]], writes=[RE]))
                    fns = []
                    for hg in range(4):
                        for ri in range(nr_):
                            rel = rlo + ri
                            fns.append(lambda e, hg=hg, E=E, rel=rel, kvh=kvh, ri=ri: e.matmul(
                                Y[:, hg * 65:(hg + 1) * 65], lhsT=E[:, rel, hg * 128:(hg + 1) * 128], rhs=vaug[s][:, rel, kvh, 0:65],
                                start=(ri == 0), stop=(ri == nr_ - 1)))
                    T.append(lambda fns=fns, RE=RE: P.group("tensor", fns, reads=[RE, R_vaug[s]], writes=[R_Y[s]]))
                    dcol = 8 + kvh * 4
                    pOd = sub(Y[:, 64:65], 0, [[65, 4]])
                    T.append(lambda pOd=pOd, dcol=dcol, kvh=kvh: P.op(V, lambda e: e.tensor_tensor(
                        out=st3[s][:, dcol:dcol + 4], in0=pOd, in1=esink[:, kvh * 4:(kvh + 1) * 4], op=ALU.add),
                        reads=[R_Y[s], R_c], writes=[R_st3[s]]))
                    T.append(lambda dcol=dcol: P.op(V, lambda e: e.reciprocal(out=st3[s][:, dcol:dcol + 4], in_=st3[s][:, dcol:dcol + 4]),
                                                    reads=[R_st3[s]], writes=[R_st3[s]]))
                    pOv = sub(Y[:, 0:1], 0, [[65, 4], [1, 64]])
                    rcv = sub(st3[s][:, dcol:dcol + 1], 0, [[1, 4], [0, 64]])
                    T.append(lambda pOv=pOv, rcv=rcv, kvh=kvh: P.op(V, lambda e: e.tensor_tensor(
                        out=ya[s][:, kvh * 256:(kvh + 1) * 256].rearrange("p (a b) -> p a b", b=64), in0=pOv, in1=rcv, op=ALU.mult),
                        reads=[R_Y[s], R_st3[s]], writes=[R_ya[s]]))
                T.append(lambda: P.op("scalar", lambda e: e.activation(out=junk3[s][:, 0:512], in_=ya[s][:], func=AF.Square, accum_out=st3[s][:, 2:3]),
                                      reads=[R_ya[s]], writes=[R_junk3[s], R_st3[s]]))
                rstd_ops(T, s, 2, 512)
                T.append(lambda: P.op(V, lambda e: e.scalar_tensor_tensor(out=ya[s][:], in0=ya[s][:], scalar=st3[s][:, 3:4], in1=nwa[:],
                                                                          op0=ALU.mult, op1=ALU.mult), reads=[R_ya[s], R_st3[s], R_c], writes=[R_ya[s]]))
                T.append(lambda: P.op("gpsimd", lambda e: e.tensor_tensor(out=mixed[s][:, 512:1024], in0=ya[s][:], in1=szs[s][:, 512:1024], op=ALU.mult),
                                      reads=[R_ya[s], R_szs[s]], writes=[R_mixed[s]]))
                T.append(lambda: P.group("tensor", [(lambda e, k=k: e.transpose(out=Xb[:, k * 128:(k + 1) * 128], in_=mixed[s][:, k * 128:(k + 1) * 128],
                                                                               identity=ident_b[:])) for k in range(8)],
                                         reads=[R_mixed[s], R_ident], writes=[R_X[s]]))
                T.append(lambda: P.op(V, lambda e: e.tensor_copy(out=mT[s][:].rearrange("p a b -> p (a b)"), in_=Xb[:, :]),
                                      reads=[R_X[s]], writes=[R_mT[s]]))
                for nb2, (BK, RB) in enumerate(((Y, R_Y[s]), (X, R_X[s]))):
                    T.append(lambda nb2=nb2, BK=BK, RB=RB: P.group("tensor", [(lambda e, k=k: e.matmul(
                        BK[:, :], lhsT=mT[s][:, k, :], rhs=w_out_bf[:, k, nb2 * 512:(nb2 + 1) * 512], start=(k == 0), stop=(k == 7))) for k in range(8)],
                        reads=[R_mT[s], R_c], writes=[RB]))
                    T.append(lambda nb2=nb2, BK=BK, RB=RB: P.op(V, lambda e: e.tensor_tensor(
                        out=rr[s][:, nb2 * 512:(nb2 + 1) * 512], in0=BK[:, :], in1=xr[l][:, nb2 * 512:(nb2 + 1) * 512], op=ALU.add),
                        reads=[RB, R_xr[l]], writes=[R_rr[s]]))
                T.append(lambda: P.op("scalar", lambda e: e.activation(out=junk3[s][:], in_=rr[s][:], func=AF.Square, accum_out=st3[s][:, 4:5]),
                                      reads=[R_rr[s]], writes=[R_junk3[s], R_st3[s]]))
                rstd_ops(T, s, 4, D)
                T.append(lambda: P.op(V, lambda e: e.scalar_tensor_tensor(out=rr[s][:], in0=rr[s][:], scalar=st3[s][:, 5:6], in1=fnw[:],
                                                                          op0=ALU.mult, op1=ALU.mult), reads=[R_rr[s], R_st3[s], R_c], writes=[R_rr[s]]))
                T.append(lambda: st_toks.append(P.dma("gpsimd", lambda e: e.dma_start(out=y[rows, :], in_=rr[s][:]), f"yo{s}", reads=[R_rr[s]], writes=[R_rr[s]])))
                return T

            sched = []
            nblk = NB if upto >= 3 else 0
            for b in range(nblk):
                L = []
                if b == 0:
                    L += loads(0) + (loads(1) if nblk > 1 else [])
                if b + 2 < nblk:
                    L += loads(b + 2)
                L += compute(b)
                n = len(L)
                for i, th in enumerate(L):
                    sched.append((b / NS + i / n, b, i, th))
            sched.sort(key=lambda t: (t[0], t[1], t[2]))
            for _, _, _, th in sched:
                th()
            P.wait_all("gpsimd", st_toks[-NS:])
        P.barrier()
        P.replay()
    return nc


PARAM_KEYS = ["norm_w", "w_in", "lam_re", "lam_im", "log_step", "b_re", "b_im", "c_re", "c_im", "d_skip", "w_glu", "b_glu",
              "ssm_norm_w", "sink", "attn_norm_w", "w_out", "rel_bias", "final_norm_w"]


def param_map(inputs):
    f = lambda a: np.ascontiguousarray(np.asarray(a, dtype=np.float32))
    m = {}
    m["norm_w"] = f(inputs["norm_w"]).reshape(1, D)
    m["w_in"] = f(inputs["w_in"]).reshape(D, DP)
    m["lam_re"] = f(inputs["lam_re"]).reshape(2, 32, 64)
    m["lam_im"] = f(inputs["lam_im"]).reshape(2, 32, 64)
    m["log_step"] = f(inputs["log_step"]).reshape(1, 64)
    m["b_re"] = f(inputs["b_re"]).reshape(2, 32, 64, 16)
    m["b_im"] = f(inputs["b_im"]).reshape(2, 32, 64, 16)
    m["c_re"] = f(inputs["c_re"]).reshape(2, 32, 16, 64)
    m["c_im"] = f(inputs["c_im"]).reshape(2, 32, 16, 64)
    m["d_skip"] = f(inputs["d_skip"]).reshape(1, 512)
    m["w_glu"] = f(inputs["w_glu"]).reshape(512, 512)
    m["b_glu"] = f(inputs["b_glu"]).reshape(1, 512)
    m["ssm_norm_w"] = f(inputs["ssm_norm_w"]).reshape(1, 512)
    m["sink"] = f(inputs["sink"]).reshape(1, 8)
    m["attn_norm_w"] = f(inputs["attn_norm_w"]).reshape(1, 512)
    m["w_out"] = f(inputs["w_out"]).reshape(D, D)
    m["rel_bias"] = f(inputs["rel_bias"]).reshape(32, 8)
    m["final_norm_w"] = f(inputs["final_norm_w"]).reshape(1, D)
    m.update(host_consts())
    return m


def kernel(**inputs):
    xp = np.asarray(inputs["x_prompt"], dtype=np.float32)
    xs = np.asarray(inputs["x_sample"], dtype=np.float32)
    n = 8
    seqs = [2048] * 4 + [4096] * 2
    nc = build_nc(seqs)
    pm = param_map(inputs)
    in_maps = []
    for i in range(n):
        xc = np.concatenate([xp[4 * i:4 * i + 4].reshape(-1, D), xs[2 * i:2 * i + 2].reshape(-1, D)], axis=0)
        m = dict(pm)
        m["x"] = np.ascontiguousarray(xc)
        in_maps.append(m)
    res = run_bass_kernel_spmd(nc, in_maps, core_ids=list(range(n)))
    yp = np.empty_like(xp)
    ys = np.empty_like(xs)
    for i in range(n):
        yc = np.asarray(res.results[i]["y"], dtype=np.float32)
        yp[4 * i:4 * i + 4] = yc[:8192].reshape(4, 2048, D)
        ys[2 * i:2 * i + 2] = yc[8192:].reshape(2, 4096, D)
    return (yp, ys)
```
